# Optimizing a Trainium2 kernel written in Bass

```python
import jax, jax.numpy as jnp
from jax import lax
import numpy as np

D_MODEL = 1024
BATCH = 8
SEQ = 2048
DEPTH = 1
DEC_BATCH = 128
DEC_SEQ = 4
PAST_LEN = 16384
PAGE_SIZE = 128

N_META = 16
D_RWKV = D_MODEL
HEAD_SIZE = 64
N_HEADS = D_RWKV // HEAD_SIZE
D_DECAY_LORA = 64
D_AAA_LORA = 64
D_GATE_LORA = 160
D_CONV = D_MODEL
CONV_W = 3
D_FF = 2816
RWKV_PROJ = 3 * D_RWKV + D_DECAY_LORA + D_AAA_LORA + D_GATE_LORA
SC_PROJ = 3 * D_CONV
GATE_PROJ = 2 * D_MODEL
P_TOTAL = RWKV_PROJ + SC_PROJ + GATE_PROJ
RMS_EPS = 1e-6
GN_EPS = 64e-5

kernel_name = "rwkv7_shortconv_gated_hybrid_step"

RWKV_SPLITS = [D_RWKV, 2 * D_RWKV, 3 * D_RWKV, 3 * D_RWKV + D_DECAY_LORA,
               3 * D_RWKV + D_DECAY_LORA + D_AAA_LORA]


def rms_norm(x, g):
    xf = x.astype(jnp.float32)
    y = xf * lax.rsqrt(jnp.mean(xf * xf, axis=-1, keepdims=True) + RMS_EPS)
    return (y * g.astype(jnp.float32)).astype(x.dtype)


def causal_dwconv(u, buf, w):
    T = u.shape[1]
    full = jnp.concatenate([buf.astype(u.dtype), u], axis=1)
    out = full[:, 0:T] * w[0]
    for i in range(1, CONV_W):
        out = out + full[:, i:i + T] * w[i]
    return out, full[:, -(CONV_W - 1):]


def wkv7_scan(S0, r, w, k, v, a, b):
    def step(S, inp):
        r_t, w_t, k_t, v_t, a_t, b_t = inp
        sa = jnp.einsum('bhij,bhj->bhi', S, a_t)
        S = (S * w_t[:, :, None, :] + sa[..., None] * b_t[:, :, None, :]
             + v_t[..., None] * k_t[:, :, None, :])
        y = jnp.einsum('bhij,bhj->bhi', S, r_t)
        return S, y
    seq = tuple(jnp.moveaxis(t, 1, 0) for t in (r, w, k, v, a, b))
    S, ys = lax.scan(step, S0, seq)
    return jnp.moveaxis(ys, 0, 1), S


def hybrid_layer(x, s_wkv, s_shift, s_sc, s_ffn, norm1_g, w_in, b_gate, mu_shift, w0,
                 w_decay_up, a0, w_aaa_up, w_gate_up, k_k, k_a, r_k, lnx_g, lnx_b,
                 w_branch_rwkv, w_branch_sc, conv_sc, w_out, norm2_g, w_up, conv_ffn, w_down):
    Bsz, T, _ = x.shape
    xn = rms_norm(x, norm1_g)
    p = xn @ w_in
    p_rwkv = p[..., :RWKV_PROJ]
    p_sc = p[..., RWKV_PROJ:RWKV_PROJ + SC_PROJ]
    p_gate = p[..., RWKV_PROJ + SC_PROJ:] + b_gate

    prev = jnp.concatenate([s_shift[:, None].astype(p.dtype), p_rwkv[:, :-1]], axis=1)
    xs = p_rwkv + (prev - p_rwkv) * mu_shift
    new_shift = p_rwkv[:, -1]
    r, k, v, xw, xa, xg = jnp.split(xs, RWKV_SPLITS, axis=-1)
    w = -jax.nn.softplus(-(w0 + jnp.tanh(xw) @ w_decay_up)) - 0.5
    a = jax.nn.sigmoid(a0 + xa @ w_aaa_up)
    g = jax.nn.sigmoid(xg) @ w_gate_up
    kk = (k * k_k).reshape(Bsz, T, N_HEADS, HEAD_SIZE).astype(jnp.float32)
    kk = kk / jnp.maximum(jnp.linalg.norm(kk, axis=-1, keepdims=True), 1e-12)
    k = k * (1 + (a - 1) * k_a)
    heads = lambda t: t.reshape(Bsz, T, N_HEADS, HEAD_SIZE).astype(jnp.float32)
    r_h, k_h, v_h, a_h = heads(r), heads(k), heads(v), heads(a)
    decay = jnp.exp(-jnp.exp(heads(w)))
    y, S = wkv7_scan(s_wkv.astype(jnp.float32), r_h, decay, k_h, v_h, -kk, kk * a_h)
    mu = jnp.mean(y, axis=-1, keepdims=True)
    var = jnp.mean(jnp.square(y - mu), axis=-1, keepdims=True)
    yn = ((y - mu) * lax.rsqrt(var + GN_EPS)).reshape(Bsz, T, D_RWKV)
    yn = yn * lnx_g.astype(jnp.float32) + lnx_b.astype(jnp.float32)
    bonus = jnp.sum(r_h * k_h * r_k.astype(jnp.float32), axis=-1, keepdims=True) * v_h
    o_a = ((yn + bonus.reshape(Bsz, T, D_RWKV)).astype(x.dtype) * g) @ w_branch_rwkv

    h, Bg, Cg = jnp.split(p_sc, 3, axis=-1)
    conv_out, new_sc = causal_dwconv(Cg * h, s_sc, conv_sc)
    o_b = (Bg * conv_out) @ w_branch_sc

    ga, gb = jnp.split(jax.nn.sigmoid(p_gate), 2, axis=-1)
    x = x + (ga * o_a + gb * o_b) @ w_out

    xn2 = rms_norm(x, norm2_g)
    up = xn2 @ w_up
    upc, new_ffn = causal_dwconv(up, s_ffn, conv_ffn)
    gate, val = jnp.split(upc, 2, axis=-1)
    x = x + (jax.nn.silu(gate) * val) @ w_down
    return x, S.astype(s_wkv.dtype), new_shift, new_sc, new_ffn


def trunk(x, s_wkv, s_shift, s_sc, s_ffn, layer_params, final_norm_g):
    new_wkv, new_shift, new_sc, new_ffn = [], [], [], []
    for l in range(DEPTH):
        x, a, b, c, d = hybrid_layer(x, s_wkv[l], s_shift[l], s_sc[l], s_ffn[l],
                                     *[prm[l] for prm in layer_params])
        new_wkv.append(a); new_shift.append(b); new_sc.append(c); new_ffn.append(d)
    y = rms_norm(x, final_norm_g)
    return y, jnp.stack(new_wkv), jnp.stack(new_shift), jnp.stack(new_sc), jnp.stack(new_ffn)


def setup_inputs(seed: int = 0) -> dict:
    key = jax.random.key(seed)
    ks = iter(jax.random.split(key, 40))
    nrm = lambda shape, s: jax.random.normal(next(ks), shape, jnp.float32) * s
    L = DEPTH
    return {
        "x_prompt": nrm((BATCH, SEQ, D_MODEL), 1.0),
        "x_sample": nrm((DEC_BATCH, DEC_SEQ, D_MODEL), 1.0),
        "state_wkv": nrm((L, DEC_BATCH, N_HEADS, HEAD_SIZE, HEAD_SIZE), 0.1),
        "state_shift": nrm((L, DEC_BATCH, RWKV_PROJ), 1.0),
        "state_sc_conv": nrm((L, DEC_BATCH, CONV_W - 1, D_CONV), 1.0),
        "state_ffn_conv": nrm((L, DEC_BATCH, CONV_W - 1, 2 * D_FF), 1.0),
        "meta_tokens": nrm((N_META, D_MODEL), 1.0),
        "norm1_g": 1.0 + nrm((L, D_MODEL), 0.02),
        "w_in": nrm((L, D_MODEL, P_TOTAL), D_MODEL ** -0.5),
        "b_gate": nrm((L, GATE_PROJ), 0.02),
        "mu_shift": jax.random.uniform(next(ks), (L, RWKV_PROJ), jnp.float32),
        "w0": nrm((L, D_RWKV), 0.5) - 0.5,
        "w_decay_up": nrm((L, D_DECAY_LORA, D_RWKV), 0.1 * D_DECAY_LORA ** -0.5),
        "a0": nrm((L, D_RWKV), 0.1),
        "w_aaa_up": nrm((L, D_AAA_LORA, D_RWKV), 0.1 * D_AAA_LORA ** -0.5),
        "w_gate_up": nrm((L, D_GATE_LORA, D_RWKV), D_GATE_LORA ** -0.5),
        "k_k": 0.85 + nrm((L, D_RWKV), 0.02),
        "k_a": 1.0 + nrm((L, D_RWKV), 0.02),
        "r_k": nrm((L, N_HEADS, HEAD_SIZE), 0.1),
        "lnx_g": 1.0 + nrm((L, D_RWKV), 0.02),
        "lnx_b": nrm((L, D_RWKV), 0.02),
        "w_branch_rwkv": nrm((L, D_RWKV, D_MODEL), D_RWKV ** -0.5),
        "w_branch_sc": nrm((L, D_CONV, D_MODEL), D_CONV ** -0.5),
        "conv_sc": nrm((L, CONV_W, D_CONV), CONV_W ** -0.5),
        "w_out": nrm((L, D_MODEL, D_MODEL), 0.5 * D_MODEL ** -0.5),
        "norm2_g": 1.0 + nrm((L, D_MODEL), 0.02),
        "w_up": nrm((L, D_MODEL, 2 * D_FF), D_MODEL ** -0.5),
        "conv_ffn": nrm((L, CONV_W, 2 * D_FF), CONV_W ** -0.5),
        "w_down": nrm((L, D_FF, D_MODEL), 0.5 * D_FF ** -0.5),
        "final_norm_g": 1.0 + nrm((D_MODEL,), 0.02),
    }


def reference(x_prompt, x_sample, state_wkv, state_shift, state_sc_conv, state_ffn_conv,
              meta_tokens, norm1_g, w_in, b_gate, mu_shift, w0, w_decay_up, a0, w_aaa_up,
              w_gate_up, k_k, k_a, r_k, lnx_g, lnx_b, w_branch_rwkv, w_branch_sc, conv_sc,
              w_out, norm2_g, w_up, conv_ffn, w_down, final_norm_g):
    layer_params = (norm1_g, w_in, b_gate, mu_shift, w0, w_decay_up, a0, w_aaa_up, w_gate_up,
                    k_k, k_a, r_k, lnx_g, lnx_b, w_branch_rwkv, w_branch_sc, conv_sc, w_out,
                    norm2_g, w_up, conv_ffn, w_down)
    dt = x_prompt.dtype
    Bp = x_prompt.shape[0]
    meta = jnp.broadcast_to(meta_tokens.astype(dt)[None], (Bp, N_META, D_MODEL))
    xp = jnp.concatenate([meta, x_prompt], axis=1)
    z_wkv = jnp.zeros((DEPTH, Bp, N_HEADS, HEAD_SIZE, HEAD_SIZE), dt)
    z_shift = jnp.zeros((DEPTH, Bp, RWKV_PROJ), dt)
    z_sc = jnp.zeros((DEPTH, Bp, CONV_W - 1, D_CONV), dt)
    z_ffn = jnp.zeros((DEPTH, Bp, CONV_W - 1, 2 * D_FF), dt)
    yp, wkv_p, shift_p, sc_p, ffn_p = trunk(xp, z_wkv, z_shift, z_sc, z_ffn, layer_params, final_norm_g)
    y_prompt = yp[:, N_META:]
    y_sample, wkv_s, shift_s, sc_s, ffn_s = trunk(x_sample, state_wkv, state_shift, state_sc_conv,
                                                  state_ffn_conv, layer_params, final_norm_g)
    return (y_prompt, y_sample, wkv_p, wkv_s, shift_p, shift_s, sc_p, sc_s, ffn_p, ffn_s)
```

```python
import numpy as np
import concourse.bass as bass
import concourse.mybir as mybir
from concourse.bass_utils import run_bass_kernel_spmd

F32 = mybir.dt.float32
F32R = mybir.dt.float32r
BF16 = mybir.dt.bfloat16
AF = mybir.ActivationFunctionType
ALU = mybir.AluOpType
AX = mybir.AxisListType

D = 1024
NH = 16
DFF = 2816
RW = 3360
PT = 8480
SEQ = 2048
NMETA = 16
NPOS = SEQ + NMETA
NW = 384
EXPM05 = 0.6065306597126334
RMS_EPS = 1e-6
GN_EPS = 64e-5

TILES = [
    (0, [16, 64, 64, 64, 64, 64], False),
    (336, [64] * 6, False),
    (720, [64] * 6, False),
    (1104, [64] * 6, False),
    (1488, [64] * 6, False),
    (1872, [64] * 3, True),
]
NSEG = 16
SL = 4

C_MU = 0
C_W0 = 27
C_A0 = 35
C_KK = 43
C_KA = 51
C_RK = 59
C_LG = 67
C_LB = 75
C_BG = 83
C_CS = 99
C_CF = 123
C_OMKA = 255
C_G1 = 264
C_G2 = 272
NCOL = 280


class Op:
    __slots__ = ("eng", "semkey", "val", "dma")

    def __init__(self, eng, semkey, val, dma):
        self.eng, self.semkey, self.val, self.dma = eng, semkey, val, dma


class Sched:
    ENG = ["pe", "act", "dve", "pool", "sp"]
    NDS = 20
    POOLS = {"sp": (0, 12), "pool": (12, 20), "act": (12, 20)}

    def __init__(self):
        self.streams = {e: [] for e in self.ENG}
        self.cnt = {e: 0 for e in self.ENG}
        self.lastw = {}
        self.readers = {}
        self.waited = {}
        self.rr = {"sp": 0, "pool": 0, "act": 0}
        self.dval = [0] * self.NDS
        self.flag = set()

    def op(self, eng, fn, reads=(), writes=(), dma=False):
        psr = [n for n in reads if isinstance(n, tuple) and n[0] == "ps"]
        if psr:
            reads = [n for n in reads if n not in psr]
            writes = list(writes) + [n for n in psr if n not in writes]
        deps = []
        raw = set()
        for n in reads:
            w = self.lastw.get(n)
            if w is not None:
                deps.append(w)
                raw.add(id(w))
        for n in writes:
            w = self.lastw.get(n)
            if w is not None:
                deps.append(w)
            deps.extend(self.readers.get(n, ()))
        if dma:
            lo, hi = self.POOLS[eng]
            k = lo + self.rr[eng]
            self.rr[eng] = (self.rr[eng] + 1) % (hi - lo)
            semkey = ("d", k)
            if self.dval[k] > 0:
                deps.append(Op(None, semkey, self.dval[k], True))
            self.dval[k] += 16
            me = Op(eng, semkey, self.dval[k], True)
        else:
            self.cnt[eng] += 1
            me = Op(eng, ("c", eng), self.cnt[eng], False)
        waits = []
        for d in deps:
            if (not d.dma) and d.eng == eng and eng == "pe":
                continue
            key = (eng, d.semkey)
            if self.waited.get(key, 0) >= d.val:
                continue
            self.waited[key] = d.val
            waits.append((d.semkey, d.val, d.dma))
            if not d.dma:
                self.flag.add((d.semkey, d.val))
        self.streams[eng].append((fn, waits, me))
        for n in reads:
            self.readers.setdefault(n, []).append(me)
        for n in writes:
            self.lastw[n] = me
            self.readers[n] = []
        return me


def build_program(job_order=None):
    nc = bass.Bass("TRN2", target_bir_lowering=False)
    nc.dge_precook = False
    S = Sched()

    def din(name, shape, dt=F32):
        return nc.dram_tensor(name, list(shape), dt, kind="ExternalInput").ap()

    def dout(name, shape):
        return nc.dram_tensor(name, list(shape), F32, kind="ExternalOutput").ap()

    xp = din("xp", [SEQ, D])
    xs_in = din("xs", [64, D])
    meta = din("meta", [NMETA, D])
    st_wkv = din("st_wkv", [NSEG, NH, 64, 64])
    st_shift = din("st_shift", [NSEG, RW])
    st_sc = din("st_sc", [2 * NSEG, D])
    st_ffn = din("st_ffn", [2 * NSEG, 2 * DFF])
    w_in = din("w_in", [D, PT])
    w_br = din("w_br", [D, D])
    w_bs = din("w_bs", [D, D])
    w_out = din("w_out", [D, D])
    w_up = din("w_up", [D, 2 * DFF])
    w_down = din("w_down", [DFF, D])
    lora_pk = din("lora_pk", [8, 128, 3, 128])
    cols_in = din("cols", [128, NCOL])
    gbc_in = din("gbc", [128, D])
    ident_in = din("ident", [128, 128])
    bones_in = din("bones", [128, 128])
    maska_in = din("maska", [128, 512])
    rmask_in = din("rmask", [128, 3, NW])
    maskas_in = din("maskas", [128, 512])
    segmask_in = din("segmask", [128, NSEG])

    NJT = 80
    wbf = nc.dram_tensor("wbf", [NJT, 128, 8 * 256], BF16, kind="Internal").ap()
    y_p = dout("y_p", [SEQ, D])
    y_s = dout("y_s", [64, D])
    wkv_p = dout("wkv_p", [NH, 64, 64])
    wkv_s = dout("wkv_s", [NSEG, NH, 64, 64])
    shift_o = dout("shift_o", [17, RW])
    sc_o = dout("sc_o", [34, D])
    ffn_o = dout("ffn_o", [34, 2 * DFF])

    import contextlib
    es = contextlib.ExitStack()
    with es:
        def sb(name, shape, dt=F32):
            return es.enter_context(nc.sbuf_tensor(name, list(shape), dt))

        x_tm = sb("x_tm", [128, 3, D])
        tm_scr = sb("tm_scr", [128, D])
        xn_fm = sb("xn_fm", [128, 8, NW], BF16)
        lor_x = sb("lor_x", [128, 3, NW])
        reg1 = sb("reg1", [128, 24, NW], BF16)
        NR = 7
        NST = 3
        wst = sb("wst", [128, NST, 8, 256])
        wring = sb("wring", [128, NR, 8, 256], BF16)
        NTMP = 29
        tmps = sb("tmps", [128, NTMP, NW])
        lor_p = tmps[:, 18:21, :]
        blk = sb("blk", [128, 2, 4, 6, 128], BF16)
        amat = sb("amat", [128, 2, 512], BF16)
        msb = sb("msb", [128, 2, 2, 128], BF16)
        pq = sb("pq", [128, 2, 2, 256], BF16)
        tmsb = sb("tmsb", [128, 2, 384], BF16)
        xsb = sb("xsb", [128, 4, 128], BF16)
        usb = sb("usb", [128, 4, 128], BF16)
        hblk = sb("hblk", [128, 8, 128])
        hbb = sb("hbb", [128, 8, 128], BF16)
        hsbb = sb("hsbb", [128, 8, 128], BF16)
        htmp = sb("htmp", [128, 4, 128])
        rtb = sb("rtb", [128, 2, NW], BF16)
        identb = sb("identb", [128, 128], BF16)
        sld = sb("sld", [128, 8, 128])
        hsb = sb("hsb", [128, 8, 128])
        shc = sb("shc", [128, 27, 17])
        scc = sb("scc", [128, 8, 17, 2])
        ffc = sb("ffc", [128, 44, 17, 2])
        lwp = sb("lwp", [128, 2, 3, 128])
        cols = sb("cols_sb", [128, NCOL])
        gbc = sb("gbc_sb", [128, D])
        ident = sb("ident_sb", [128, 128])
        bones = sb("bones_sb", [128, 128])
        maska = sb("maska_sb", [128, 512])
        rmask = sb("rmask_sb", [128, 3, NW])
        maskas = sb("maskas_sb", [128, 512])
        segmask = sb("segmask_sb", [128, NSEG])
        vgs = sb("vgs", [128, 4, 128], BF16)
        stat = sb("stat", [128, 16])
        stld = tm_scr

        ps = [es.enter_context(nc.psum_tensor("ps%d" % i, [128, 512], F32)) for i in range(8)]

        sems = {}
        for e in Sched.ENG:
            sems[("c", e)] = es.enter_context(nc.semaphore("c_" + e))
        for k in range(Sched.NDS):
            sems[("d", k)] = es.enter_context(nc.semaphore("d_%d" % k))

        def dma(q, out, in_, reads, writes):
            return S.op(q, lambda e: e.dma_start(out=out, in_=in_), reads, writes, dma=True)

        def mm(out, lhsT, rhs, start, stop, reads, writes):
            return S.op("pe", lambda e: e.matmul(out, lhsT, rhs, start=start, stop=stop), reads, writes)

        def tr(out, in_, reads, writes):
            P_ = in_.shape[0]
            idt = identb if in_.dtype == BF16 else ident
            return S.op("pe", lambda e: e.transpose(out, in_, idt[0:P_, 0:P_]), list(reads) + ["ident", "identb"], writes)

        def act(out, in_, func, reads, writes, bias=None, scale=None):
            kw = {}
            if bias is not None:
                kw["bias"] = bias
            if scale is not None:
                kw["scale"] = scale
            return S.op("act", lambda e: e.activation(out, in_, func, **kw), reads, writes)

        def tt(eng, out, in0, in1, op, reads, writes):
            return S.op(eng, lambda e: e.tensor_tensor(out, in0, in1, op), reads, writes)

        def ts(eng, out, in0, s1, s2, op0, op1, reads, writes):
            if s2 is None:
                return S.op(eng, lambda e: e.tensor_scalar(out, in0, s1, None, op0), reads, writes)
            return S.op(eng, lambda e: e.tensor_scalar(out, in0, s1, s2, op0, op1), reads, writes)

        def stt(out, in0, sc, in1, op0, op1, reads, writes):
            return S.op("dve", lambda e: e.scalar_tensor_tensor(out, in0, sc, in1, op0, op1), reads, writes)

        def cp(eng, out, in_, reads, writes):
            if eng == "act":
                return act(out, in_, AF.Copy, reads, writes)
            return S.op(eng, lambda e: e.tensor_copy(out, in_), reads, writes)

        def memset(eng, ap, val, writes):
            return S.op(eng, lambda e: e.memset(ap, val), (), writes)

        def col(i):
            return cols[:, i:i + 1]

        rot = {"big": [0, 1], "tm": [2, 3], "scan0": [4, 5], "scan1": [6, 7], "wide": [0, 1, 4, 5, 6, 7]}
        rpos = {"big": 0, "tm": 0, "scan0": 0, "scan1": 0, "wide": 0}

        def bank(group):
            b = rot[group][rpos[group] % len(rot[group])]
            rpos[group] += 1
            return b

        wj = {"n": 0, "iss": 0}

        WTS = {"w_in": w_in, "w_br": w_br, "w_bs": w_bs, "w_out": w_out, "w_up": w_up, "w_down": w_down}
        WNAME = {id(v): k for k, v in WTS.items()}
        recorded = []
        JOBS = [(WTS[n], a, b, c, d) for (n, a, b, c, d) in job_order] if job_order is not None else None
        LA = 3

        def issue(j, jb=None):
            W, k0, nk, c0, ncols = JOBS[j] if jb is None else jb
            st = j % NST
            slot = j % NR
            jj = j % NJT
            scr = wbf[jj, :, 0:nk * ncols].rearrange("p (k c) -> p k c", c=ncols)
            if j < NJT:
                src = W[k0:k0 + nk * 128, c0:c0 + ncols].rearrange("(k p) c -> p k c", p=128)
                dma("sp", wst[:, st, 0:nk, 0:ncols], src, (), [("ws", st)])
                cp("act", wring[:, slot, 0:nk, 0:ncols], wst[:, st, 0:nk, 0:ncols], [("ws", st)], [("w", slot)])
                dma("pool", scr, wring[:, slot, 0:nk, 0:ncols], [("w", slot)], [("wbf", jj)])
            else:
                dma("sp", wring[:, slot, 0:nk, 0:ncols], scr, [("wbf", jj)], [("w", slot)])

        def wjob(W, k0, nk, c0, ncols):
            j = wj["n"]
            recorded.append((WNAME[id(W)], k0, nk, c0, ncols))
            if JOBS is None:
                issue(j, (W, k0, nk, c0, ncols))
            else:
                jb = JOBS[j]
                assert jb[0] is W and jb[1:] == (k0, nk, c0, ncols), (j, jb[1:], (k0, nk, c0, ncols))
                while wj["iss"] < min(len(JOBS), j + 1 + LA):
                    issue(wj["iss"])
                    wj["iss"] += 1
            wj["n"] += 1
            return j % NR

        tmp_free = list(range(NTMP))

        def T(i):
            return tmps[:, i, :]

        dma("sp", cols[:, :], cols_in, (), ["cols"])
        dma("sp", gbc[:, :], gbc_in, (), ["gbc"])
        dma("sp", ident[:, :], ident_in, (), ["ident"])
        dma("sp", bones[:, :], bones_in, (), ["bones"])
        dma("sp", maska[:, :], maska_in, (), ["maska"])
        dma("sp", rmask[:, :, :], rmask_in, (), ["rmask"])
        dma("sp", maskas[:, :], maskas_in, (), ["maskas"])
        dma("sp", segmask[:, :], segmask_in, (), ["segmask"])
        ts("dve", cols[:, C_OMKA:C_OMKA + 8], cols[:, C_KA:C_KA + 8], -1.0, 1.0, ALU.mult, ALU.add,
           ["cols"], ["cols"])
        memset("dve", blk[:, :, :, :, :], 0.0, [("blk", 0), ("blk", 1)])
        memset("dve", hblk[:, :, :], 0.0, [("H", m) for m in range(8)])
        memset("dve", hbb[:, :, :], 0.0, [("Hb", m) for m in range(8)])
        cp("dve", identb[:, :], ident[:, :], ["ident"], ["identb"])
        memset("dve", tmps[:, 0, :], 0.0, ["t0"])
        memset("dve", reg1[:, :, :], 0.0, ["reg1"])
        memset("dve", sld[:, :, :], 0.0, ["sld"])
        memset("dve", xn_fm[:, :, :], 0.0, ["xn_fm"])
        memset("dve", shc[:, :, :], 0.0, [("shc", c) for c in range(27)])
        memset("dve", scc[:, :, :, :], 0.0, [("scc", c) for c in range(8)])
        memset("dve", ffc[:, :, :, :], 0.0, [("ffc", c) for c in range(44)])
        memset("dve", x_tm[:, :, :], 0.0, [("x", b) for b in range(3)])

        for c in range(27):
            w = 128 if c < 26 else 32
            if c % 8 == 0:
                wp = min(1024, RW - c * 128)
                dma("sp", tm_scr[0:16, 0:wp], st_shift[:, c * 128:c * 128 + wp], (), ["tm_scr"])
            b = bank("tm")
            tr(ps[b][0:w, 0:16], tm_scr[0:16, (c % 8) * 128:(c % 8) * 128 + w], ["tm_scr"], [("ps", b)])
            cp("act", shc[0:w, c, 0:16], ps[b][0:w, 0:16], [("ps", b)], [("shc", c)])
        dma("sp", tm_scr[0:32, 0:D], st_sc, (), ["tm_scr"])
        for c in range(8):
            b = bank("tm")
            tr(ps[b][:, 0:32], tm_scr[0:32, c * 128:(c + 1) * 128], ["tm_scr"], [("ps", b)])
            cp("act", scc[:, c, 0:16, :], ps[b][:, 0:32].rearrange("p (s r) -> p s r", r=2),
               [("ps", b)], [("scc", c)])
        for c in range(44):
            if c % 8 == 0:
                wp = min(1024, 2 * DFF - c * 128)
                dma("sp", tm_scr[0:32, 0:wp], st_ffn[:, c * 128:c * 128 + wp], (), ["tm_scr"])
            b = bank("tm")
            tr(ps[b][:, 0:32], tm_scr[0:32, (c % 8) * 128:(c % 8 + 1) * 128], ["tm_scr"], [("ps", b)])
            cp("act", ffc[:, c, 0:16, :], ps[b][:, 0:32].rearrange("p (s r) -> p s r", r=2),
               [("ps", b)], [("ffc", c)])

        def load_x_block(ti, b):
            pos0, chunks, has_s = TILES[ti]
            npr = sum(chunks)
            c0 = b * 128
            pieces = []
            r = 0
            while r < 128:
                cpos = c0 + r
                if cpos < npr:
                    p = pos0 + cpos
                    if p < NMETA:
                        n = min(NMETA - p, 128 - r, npr - cpos)
                        pieces.append((r, n, "meta", p))
                    else:
                        n = min(128 - r, npr - cpos)
                        pieces.append((r, n, "xp", p - NMETA))
                    r += n
                elif has_s and cpos < npr + 64:
                    n = min(128 - r, npr + 64 - cpos)
                    pieces.append((r, n, "xs", cpos - npr))
                    r += n
                else:
                    break
            for (r0, n, kind, sr) in pieces:
                src = {"meta": meta, "xp": xp, "xs": xs_in}[kind]
                dma("sp", x_tm[r0:r0 + n, b, :], src[sr:sr + n, :], (), [("x", b)])
            return pieces

        def rms_to_fm(src_ap, src_names, gc0, b, dst_fm, dst_names):
            tt("dve", tm_scr[:, :], src_ap, src_ap, ALU.mult, src_names, ["tm_scr"])
            S.op("dve", lambda e: e.reduce_sum(stat[:, 0:1], tm_scr[:, :], AX.X), ["tm_scr"], ["stat0"])
            act(stat[:, 1:2], stat[:, 0:1], AF.Sqrt, ["stat0", "epsc"], ["stat1"], bias=col_eps_rms, scale=1.0 / D)
            S.op("dve", lambda e: e.reciprocal(stat[:, 2:3], stat[:, 1:2]), ["stat1"], ["stat2"])
            ts("dve", tm_scr[:, :], src_ap, stat[:, 2:3], None, ALU.mult, None,
               list(src_names) + ["stat2"], ["tm_scr"])
            for half in range(2):
                bk = bank("tm")
                for q in range(4):
                    kc = half * 4 + q
                    tr(ps[bk][:, q * 128:(q + 1) * 128], tm_scr[:, kc * 128:(kc + 1) * 128],
                       ["tm_scr"], [("ps", bk)])
                for q in range(4):
                    kc = half * 4 + q
                    act(dst_fm[:, kc, b * 128:(b + 1) * 128], ps[bk][:, q * 128:(q + 1) * 128],
                        AF.Identity, [("ps", bk), "cols"], dst_names, scale=col(gc0 + kc))

        def nscr(b):
            return tmps[:, 3 * b:3 * b + 3, :].rearrange("p a c -> p (a c)")[:, 0:D]

        def nscr_names(b):
            return ["t%d" % i for i in range(3 * b, 3 * b + 3)]

        def rms_gen(src_ap, src_names, gc0, b, dst_fm, dst_names):
            scr, SN = nscr(b), nscr_names(b)
            s0, s1, s2 = (stat[:, 3 * b + i:3 * b + i + 1] for i in range(3))
            n0, n1, n2 = (("stat", b, i) for i in range(3))
            tt("dve", scr, src_ap, src_ap, ALU.mult, src_names, SN)
            S.op("dve", lambda e: e.reduce_sum(s0, scr, AX.X), SN, [n0])
            yield
            act(s1, s0, AF.Sqrt, [n0, "epsc"], [n1], bias=col_eps_rms, scale=1.0 / D)
            yield
            S.op("dve", lambda e: e.reciprocal(s2, s1), [n1], [n2])
            ts("dve", scr, src_ap, s2, None, ALU.mult, None, list(src_names) + [n2], SN)
            yield
            for half in range(2):
                bk = bank("wide")
                for q in range(4):
                    kc = half * 4 + q
                    tr(ps[bk][:, q * 128:(q + 1) * 128], scr[:, kc * 128:(kc + 1) * 128], SN, [("ps", bk)])
                for q in range(4):
                    kc = half * 4 + q
                    act(dst_fm[:, kc, b * 128:(b + 1) * 128], ps[bk][:, q * 128:(q + 1) * 128],
                        AF.Identity, [("ps", bk), "cols"], dst_names, scale=col(gc0 + kc))
                yield

        def fin_gen(b, pieces):
            xb = x_tm[:, b, :]
            scr, SN = nscr(b), nscr_names(b)
            s0, s1, s2 = (stat[:, 3 * b + i:3 * b + i + 1] for i in range(3))
            n0, n1, n2 = (("stat", b, i) for i in range(3))
            tt("dve", scr, xb, xb, ALU.mult, [("x", b)], SN)
            S.op("dve", lambda e: e.reduce_sum(s0, scr, AX.X), SN, [n0])
            yield
            act(s1, s0, AF.Sqrt, [n0, "epsc"], [n1], bias=col_eps_rms, scale=1.0 / D)
            yield
            S.op("dve", lambda e: e.reciprocal(s2, s1), [n1], [n2])
            stt(xb, xb, s2, gbc[:, :], ALU.mult, ALU.mult, [("x", b), n2, "gbc"], [("x", b)])
            for (r0, n, kind, sr) in pieces:
                if kind == "meta":
                    continue
                dst = y_p if kind == "xp" else y_s
                dma("pool", dst[sr:sr + n, :], x_tm[r0:r0 + n, b, :], [("x", b)], ())
            yield

        epsc = sb("epsc", [128, 4])
        memset("dve", epsc[:, 0:1], RMS_EPS, ["epsc"])
        memset("dve", epsc[:, 1:2], GN_EPS, ["epsc"])
        memset("dve", epsc[:, 2:3], 1e-18, ["epsc"])
        col_eps_rms = epsc[:, 0:1]
        col_eps_gn = epsc[:, 1:2]

        def proj_fm(slot, off, xin, xin_names, N, nk=8, grp="big"):
            bk = bank(grp)
            for kc in range(nk):
                mm(ps[bk][:, 0:N], wring[:, slot, kc, off:off + 128],
                   xin[:, kc, 0:N], kc == 0, kc == nk - 1,
                   [("w", slot)] + list(xin_names), [("ps", bk)])
            return bk

        def token_shift(ti, p_ap, p_name, d_ap, d_name, out_ap, out_name, ci, nrows=128):
            pos0, chunks, has_s = TILES[ti]
            npr = sum(chunks)
            N = npr + (64 if has_s else 0)
            R = slice(0, nrows)
            tt("dve", d_ap[R, 1:npr], p_ap[R, 0:npr - 1], p_ap[R, 1:npr], ALU.subtract, [p_name], [d_name])
            tt("pool", d_ap[R, 0:1], shc[R, ci, 16:17], p_ap[R, 0:1], ALU.subtract,
               [p_name, ("shc", ci)], [d_name])
            if has_s:
                p3 = p_ap[R, npr:N].rearrange("p (s t) -> p s t", t=SL)
                d3 = d_ap[R, npr:N].rearrange("p (s t) -> p s t", t=SL)
                tt("pool", d3[:, :, 1:SL], p3[:, :, 0:SL - 1], p3[:, :, 1:SL], ALU.subtract, [p_name], [d_name])
                tt("pool", d3[:, :, 0:1], shc[R, ci, 0:16].rearrange("p (s o) -> p s o", o=1), p3[:, :, 0:1],
                   ALU.subtract, [p_name, ("shc", ci)], [d_name])
            stt(out_ap[R, 0:N], d_ap[R, 0:N], cols[R, C_MU + ci:C_MU + ci + 1], p_ap[R, 0:N],
                ALU.mult, ALU.add, [d_name, p_name, "cols"], [out_name])
            cp("pool", shc[R, ci, 16:17], p_ap[R, npr - 1:npr], [p_name], [("shc", ci)])
            if has_s:
                cp("pool", shc[R, ci, 0:16].rearrange("p (s o) -> p s o", o=1), p3[:, :, SL - 1:SL],
                   [p_name], [("shc", ci)])

        def conv3(ti, u_ap, u_name, acc_ap, acc_name, cc, cc_name, wc0, skip_first=False):
            pos0, chunks, has_s = TILES[ti]
            npr = sum(chunks)
            N = npr + (64 if has_s else 0)
            w0, w1, w2 = wc0
            if not skip_first:
                ts("dve", acc_ap[:, 0:N], u_ap[:, 0:N], w2, None, ALU.mult, None, [u_name, "cols"], [acc_name])
            stt(acc_ap[:, 1:npr], u_ap[:, 0:npr - 1], w1, acc_ap[:, 1:npr], ALU.mult, ALU.add,
                [u_name, "cols", acc_name], [acc_name])
            stt(acc_ap[:, 2:npr], u_ap[:, 0:npr - 2], w0, acc_ap[:, 2:npr], ALU.mult, ALU.add,
                [u_name, "cols", acc_name], [acc_name])
            stt(acc_ap[:, 0:1], cc[:, 16, 1:2], w1, acc_ap[:, 0:1], ALU.mult, ALU.add,
                [cc_name, "cols", acc_name], [acc_name])
            stt(acc_ap[:, 0:2], cc[:, 16, 0:2], w0, acc_ap[:, 0:2], ALU.mult, ALU.add,
                [cc_name, "cols", acc_name], [acc_name])
            if has_s:
                u3 = u_ap[:, npr:N].rearrange("p (s t) -> p s t", t=SL)
                a3 = acc_ap[:, npr:N].rearrange("p (s t) -> p s t", t=SL)
                stt(a3[:, :, 1:SL], u3[:, :, 0:SL - 1], w1, a3[:, :, 1:SL], ALU.mult, ALU.add,
                    [u_name, "cols", acc_name], [acc_name])
                stt(a3[:, :, 2:SL], u3[:, :, 0:SL - 2], w0, a3[:, :, 2:SL], ALU.mult, ALU.add,
                    [u_name, "cols", acc_name], [acc_name])
                stt(a3[:, :, 0:1], cc[:, 0:16, 1:2], w1, a3[:, :, 0:1], ALU.mult, ALU.add,
                    [cc_name, "cols", acc_name], [acc_name])
                stt(a3[:, :, 0:2], cc[:, 0:16, 0:2], w0, a3[:, :, 0:2], ALU.mult, ALU.add,
                    [cc_name, "cols", acc_name], [acc_name])
                cp("pool", cc[:, 0:16, :], u3[:, :, SL - 2:SL], [u_name], [cc_name])
            cp("pool", cc[:, 16, :], u_ap[:, npr - 2:npr], [u_name], [cc_name])

        def p1_gen(ch, par, slot, C, col0, res, mask_ap=None, mask_name="maska", nl=None):
            if mask_ap is None:
                mask_ap = maska
            at = blk[:, par, 0, slot, :]
            bt = blk[:, par, 1, slot, :]
            kt = blk[:, par, 2, slot, :]
            vb = blk[:, par, 3, slot, :]
            BL = ("blk", par)
            RT = ("rtb", par)
            rts = rtb[:, par, col0:col0 + C]
            am = amat[:, ch, :]
            AM = ("amat", ch)
            grp = "scan%d" % ch
            bA = bank(grp)
            pA = ps[bA]
            nA = ("ps", bA)
            mm(pA[:, 0:128], bt, at, True, True, [BL], [nA])
            mm(pA[:, 384:384 + C], bt, rts, True, True, [BL, RT], [nA])
            mm(pA[:, 256:384], kt, at, True, True, [BL], [nA])
            mm(pA[:, 448:448 + C], kt, rts, True, True, [BL, RT], [nA])
            mm(pA[:, 128:256], at, bt, True, True, [BL], [nA])
            if C == 64:
                tt("dve", am, pA[:, :], mask_ap[:, :], ALU.mult, [nA, mask_name], [AM])
            else:
                tt("dve", am[:, 0:384], pA[:, 0:384], mask_ap[:, 0:384], ALU.mult, [nA, mask_name], [AM])
                tt("dve", am[:, 384:384 + C], pA[:, 384:384 + C], mask_ap[:, 384:384 + C], ALU.mult,
                   [nA, mask_name], [AM])
                tt("dve", am[:, 448:448 + C], pA[:, 448:448 + C], mask_ap[:, 448:448 + C], ALU.mult,
                   [nA, mask_name], [AM])
            mi = 0
            tt("dve", msb[:, ch, mi, :], am[:, 0:128], identb[:, :], ALU.add, [AM, "identb"], [("msb", ch, mi)])
            yield
            bT = bank(grp)
            nT = ("ps", bT)
            pT = ps[bT][:, 0:192].bitcast(BF16)
            tr(pT[:, 0:128], vb, [BL], [nT])
            tr(pT[:, 128:256], bt, [BL], [nT])
            tr(pT[:, 256:384], kt, [BL], [nT])
            TM = ("tmsb", ch)
            cp("act", tmsb[:, ch, :], pT[:, 0:384], [nT], [TM])
            res["tm"] = (tmsb[:, ch, :], TM)
            yield
            if nl is None:
                nl = 1
                while (1 << nl) < C:
                    nl += 1
            Pp, Qp, pname = am[:, 0:128], am[:, 128:256], AM
            for k in range(1, nl):
                last = (k == nl - 1)
                bB = bank(grp)
                nB = ("ps", bB)
                if not last:
                    mm(ps[bB][:, 0:128], Qp, Pp, True, True, [pname], [nB])
                mm(ps[bB][:, 128:256], Pp, Qp, True, True, [pname], [nB])
                dst = pq[:, ch, k % 2, :]
                dname = ("pq", ch, k % 2)
                if not last:
                    cp("act", dst, ps[bB][:, 0:256], [nB], [dname])
                else:
                    cp("act", dst[:, 128:256], ps[bB][:, 128:256], [nB], [dname])
                Pp, Qp, pname = dst[:, 0:128], dst[:, 128:256], dname
                yield
                mm(ps[bB][:, 256:384], Qp, msb[:, ch, mi, :], True, True, [dname, ("msb", ch, mi)], [nB])
                tt("dve", msb[:, ch, 1 - mi, :], msb[:, ch, mi, :], ps[bB][:, 256:384], ALU.add,
                   [nB, ("msb", ch, mi)], [("msb", ch, 1 - mi)])
                mi = 1 - mi
                yield
            res["M"] = (msb[:, ch, mi, :], ("msb", ch, mi))
            res["am"] = (am, AM)
            yield

        def p2_gen(ch, par, slot, C, col0, res, H_ap, H_name, Hb_ap, Hb_name, y_ap, y_name, W_ap, W_name,
                   lc=0, rowmask=None):
            at = blk[:, par, 0, slot, :]
            BL = ("blk", par)
            RT = ("rtb", par)
            rts = rtb[:, par, col0:col0 + C]
            am, AM = res["am"]
            tmv, TM = res["tm"]
            M_ap, M_name = res["M"]
            vtm, btm, ktm = tmv[:, 0:128], tmv[:, 128:256], tmv[:, 256:384]
            grp = ("scan0", "scan1", "tm", "big")[ch]
            bX = bank(grp)
            nX = ("ps", bX)
            pX = ps[bX]
            XS, US, HT = ("xsb", ch), ("usb", ch), ("htmp", ch)
            mm(pX[:, 0:128], at, Hb_ap, True, False, [BL, Hb_name], [nX])
            mm(pX[:, 0:128], am[:, 256:384], vtm, False, True, [AM, TM], [nX])
            cp("act", xsb[:, ch, :], pX[:, 0:128], [nX], [XS])
            if rowmask is not None:
                ts("pool", vgs[:, ch, :], vtm, rowmask, None, ALU.mult, None, [TM, "segmask"], [("vgs", ch)])
                vH, vHn = vgs[:, ch, :], ("vgs", ch)
            else:
                vH, vHn = vtm, TM
            yield
            mm(pX[:, 128:256], M_ap, xsb[:, ch, :], True, True, [M_name, XS], [nX])
            if rowmask is not None:
                ts("dve", usb[:, ch, :], pX[:, 128:256], rowmask, None, ALU.mult, None, [nX, "segmask"], [US])
            else:
                cp("dve", usb[:, ch, :], pX[:, 128:256], [nX], [US])
            yield
            mm(pX[:, 384:512], btm, usb[:, ch, :], True, False, [TM, US], [nX])
            mm(pX[:, 384:512], ktm, vH, False, True, [TM, vHn], [nX])
            bY = bank(grp)
            nY = ("ps", bY)
            pY = ps[bY]
            mm(pY[:, 0:C], Hb_ap, rts, True, False, [Hb_name, RT], [nY])
            mm(pY[:, 0:C], usb[:, ch, :], am[:, 384 + lc:384 + lc + C], False, False, [US, AM], [nY])
            mm(pY[:, 0:C], vtm, am[:, 448 + lc:448 + lc + C], False, True, [TM, AM], [nY])
            Wc = W_ap[:, col0 + C - 1:col0 + C]
            tt("dve", htmp[:, ch, :], pX[:, 384:512], H_ap, ALU.add, [nX, H_name], [HT])
            act(Hb_ap, htmp[:, ch, :], AF.Identity, [HT, W_name], [Hb_name], scale=Wc)
            act(H_ap, htmp[:, ch, :], AF.Identity, [HT, W_name], [H_name], scale=Wc)
            cp("act", y_ap[:, col0:col0 + C], pY[:, 0:C], [nY], [y_name])
            yield

        def scan_gen(ch, par, slot, C, col0, H_ap, H_name, Hb_ap, Hb_name, y_ap, y_name, W_ap, W_name):
            res = {}
            yield from p1_gen(ch, par, slot, C, col0, res)
            yield from p2_gen(ch, par, slot, C, col0, res, H_ap, H_name, Hb_ap, Hb_name, y_ap, y_name, W_ap, W_name)

        def run_gens(gens):
            gens = list(gens)
            while gens:
                for g in list(gens):
                    try:
                        next(g)
                    except StopIteration:
                        gens.remove(g)

        for ti, (pos0, chunks, has_s) in enumerate(TILES):
            npr = sum(chunks)
            N = npr + (64 if has_s else 0)
            nblk = (N + 127) // 128
            rmi = 0 if ti == 0 else (2 if has_s else 1)
            store_pieces = []
            for b in range(nblk):
                store_pieces.append(load_x_block(ti, b))
            run_gens([rms_gen(x_tm[:, b, :], [("x", b)], C_G1, b, xn_fm, ["xn_fm"]) for b in range(nblk)])
            XN = ["xn_fm"]
            for half in range(2):
                c0 = 3072 + half * 256
                nc_ = 256 if half == 0 else 128
                slot = wjob(w_in, 0, 8, c0, nc_)
                for q in range(nc_ // 128):
                    c = half * 2 + q
                    bk = proj_fm(slot, q * 128, xn_fm, XN, N)
                    cp("act", lor_p[:, c, 0:N], ps[bk][:, 0:N], [("ps", bk)], [("lor_p", c)])
                    token_shift(ti, lor_p[:, c, :], ("lor_p", c), T(0), "t0", lor_x[:, c, :], ("lor_x", c), 24 + c,
                                nrows=128 if c < 2 else 32)
            act(lor_x[0:64, 0, 0:N], lor_x[0:64, 0, 0:N], AF.Tanh, [("lor_x", 0)], [("lor_x", 0)])
            act(lor_x[:, 1, 0:N], lor_x[:, 1, 0:N], AF.Sigmoid, [("lor_x", 1)], [("lor_x", 1)])
            act(lor_x[0:32, 2, 0:N], lor_x[0:32, 2, 0:N], AF.Sigmoid, [("lor_x", 2)], [("lor_x", 2)])
            z_fm = reg1[:, 0:8, :]
            WT = [13, 21]
            YT = [3, 22]
            BN = [17, 23]
            GT = [9, 24]

            def tn(i):
                return "t%d" % i

            IDS = [(1, 2, 3, 0, 4, 5, 6), (25, 26, 27, 28, 18, 19, 20)]

            def prep_ls(m, par):
                Wt, bon, gt = WT[par], BN[par], GT[par]
                lwi = m % 2
                dma("sp", lwp[:, lwi, :, :], lora_pk[m], (), [("lw", lwi)])
                LW = ("lw", lwi)
                bk = bank("big")
                mm(ps[bk][:, 0:N], lwp[0:64, lwi, 0, :], lor_x[0:64, 0, 0:N], True, True,
                   [LW, ("lor_x", 0)], [("ps", bk)])
                act(T(7)[:, 0:N], ps[bk][:, 0:N], AF.Sigmoid, [("ps", bk), "cols"], ["t7"], bias=col(C_W0 + m))
                bk = bank("big")
                mm(ps[bk][:, 0:N], lwp[64:128, lwi, 0, :], lor_x[64:128, 0, 0:N], True, True,
                   [LW, ("lor_x", 0)], [("ps", bk)])
                act(T(8)[:, 0:N], ps[bk][:, 0:N], AF.Sigmoid, [("ps", bk), "cols"], ["t8"], bias=col(C_A0 + m))
                bk = bank("big")
                mm(ps[bk][:, 0:N], lwp[:, lwi, 1, :], lor_x[:, 1, 0:N], True, False,
                   [LW, ("lor_x", 1)], [("ps", bk)])
                mm(ps[bk][:, 0:N], lwp[0:32, lwi, 2, :], lor_x[0:32, 2, 0:N], False, True,
                   [LW, ("lor_x", 2)], [("ps", bk)])
                cp("act", T(gt)[:, 0:N], ps[bk][:, 0:N], [("ps", bk)], [tn(gt)])
                S.op("dve", lambda e, o=T(12)[:, 0:N], d0=rmask[:, rmi, 0:N], d1=T(7)[:, 0:N]:
                     e.tensor_tensor_scan(o, d0, d1, 0.0, ALU.mult, ALU.add), ["rmask", "t7"], ["t12"])
                act(T(Wt)[:, 0:N], T(12)[:, 0:N], AF.Exp, ["t12"], [tn(Wt)], scale=-EXPM05)
                act(T(14)[:, 0:N], T(12)[:, 0:N], AF.Exp, ["t12"], ["t14"], scale=EXPM05)
                tt("pool", T(15)[:, 0:N], T(12)[:, 0:N], T(7)[:, 0:N], ALU.subtract, ["t12", "t7"], ["t15"])
                act(T(15)[:, 0:N], T(15)[:, 0:N], AF.Exp, ["t15"], ["t15"], scale=-EXPM05)

            def prep_front(m, pm, slots):
                ids = IDS[pm]
                for which in range(3):
                    bk = proj_fm(slots[which], pm * 128, xn_fm, XN, N)
                    raw = T(ids[which])
                    rn_ = tn(ids[which])
                    cp("act", raw[:, 0:N], ps[bk][:, 0:N], [("ps", bk)], [rn_])
                    token_shift(ti, raw, rn_, T(ids[3]), tn(ids[3]), T(ids[4 + which]), tn(ids[4 + which]),
                                which * 8 + m)

            def prep_back(m, par, pm):
                Wt, bon, gt = WT[par], BN[par], GT[par]
                rid, kid, vid = IDS[pm][4:7]
                ts("dve", T(10)[:, 0:N], T(kid)[:, 0:N], col(C_KK + m), None, ALU.mult, None, [tn(kid), "cols"], ["t10"])
                tt("pool", T(11)[:, 0:N], T(10)[:, 0:N], T(10)[:, 0:N], ALU.mult, ["t10"], ["t11"])
                bk = bank("big")
                mm(ps[bk][:, 0:N], bones[:, :], T(11)[:, 0:N], True, True, ["bones", "t11"], [("ps", bk)])
                act(T(11)[:, 0:N], ps[bk][:, 0:N], AF.Ln, [("ps", bk), "epsc"], ["t11"], bias=epsc[:, 2:3])
                act(T(11)[:, 0:N], T(11)[:, 0:N], AF.Exp, ["t11"], ["t11"], scale=-0.5)
                stt(T(10)[:, 0:N], T(10)[:, 0:N], -1.0, T(11)[:, 0:N], ALU.mult, ALU.mult, ["t10", "t11"], ["t10"])
                ts("dve", T(11)[:, 0:N], T(8)[:, 0:N], col(C_KA + m), col(C_OMKA + m), ALU.mult, ALU.add,
                   ["t8", "cols"], ["t11"])
                tt("dve", T(kid)[:, 0:N], T(kid)[:, 0:N], T(11)[:, 0:N], ALU.mult, [tn(kid), "t11"], [tn(kid)])
                stt(T(11)[:, 0:N], T(10)[:, 0:N], -1.0, T(8)[:, 0:N], ALU.mult, ALU.mult, ["t10", "t8"], ["t11"])
                tt("dve", rtb[:, par, 0:N], T(rid)[:, 0:N], T(Wt)[:, 0:N], ALU.mult, [tn(rid), tn(Wt)], [("rtb", par)])
                stt(T(bon)[:, 0:N], T(rid)[:, 0:N], col(C_RK + m), T(kid)[:, 0:N], ALU.mult, ALU.mult,
                    [tn(rid), "cols", tn(kid)], [tn(bon)])
                bk = bank("big")
                mm(ps[bk][:, 0:N], bones[:, :], T(bon)[:, 0:N], True, True, ["bones", tn(bon)], [("ps", bk)])
                tt("dve", T(bon)[:, 0:N], ps[bk][:, 0:N], T(vid)[:, 0:N], ALU.mult, [("ps", bk), tn(vid)], [tn(bon)])

            def fill(par, regions, s0, zero_first, kid=5, vid=6):
                BL = ("blk", par)
                if zero_first:
                    memset("pool", blk[:, par, :, :, :], 0.0, [BL])
                for (c0_, nch, C) in regions:
                    for hh in range(2):
                        P_ = slice(hh * 64, hh * 64 + 64)

                        def v3(tix):
                            return T(tix)[P_, c0_:c0_ + nch * C].rearrange("p (n c) -> p n c", c=C)

                        def o3(bi):
                            return blk[P_, par, bi, s0:s0 + nch, hh * 64:hh * 64 + C]
                        tt("dve", o3(0), v3(10), v3(15), ALU.mult, ["t10", "t15"], [BL])
                        tt("dve", o3(1), v3(11), v3(14), ALU.mult, ["t11", "t14"], [BL])
                        tt("pool", o3(2), v3(kid), v3(14), ALU.mult, [tn(kid), "t14"], [BL])
                        cp("pool", o3(3), v3(vid), [tn(vid)], [BL])
                    s0 += nch

            def prompt_gen(m, par):
                c_ = 0
                for si, C in enumerate(chunks):
                    yield from scan_gen(par, par, si, C, c_, hblk[:, m, :], ("H", m), hbb[:, m, :], ("Hb", m),
                                        T(YT[par]), tn(YT[par]), T(WT[par]), tn(WT[par]))
                    c_ += C

            def sample_part(m, par):
                HS = [("hsb", i) for i in range(8)]
                HSB = [("hsbb", i) for i in range(8)]
                slot_s = len(chunks)
                res = {}
                for _ in p1_gen(0, par, slot_s, 64, npr, res, mask_ap=maskas, mask_name="maskas", nl=2):
                    pass

                def p2_chain(ch, qs, sb0):
                    for q in qs:
                        sg = sb0 + q
                        yield from p2_gen(ch, par, slot_s, SL, npr + sg * SL, res, hsb[:, q, :], ("hsb", q),
                                          hsbb[:, q, :], ("hsbb", q), T(YT[par]), tn(YT[par]),
                                          T(WT[par]), tn(WT[par]), lc=sg * SL, rowmask=segmask[:, sg:sg + 1])
                for hv in range(2):
                    sb0 = hv * 8
                    for hh in range(2):
                        dma("sp", sld[hh * 64:hh * 64 + 64, :, hh * 64:hh * 64 + 64],
                            st_wkv[sb0:sb0 + 8, 2 * m + hh, :, :].rearrange("s i j -> i s j"), (), ["sld"])
                    for g4 in range(2):
                        bk = bank("tm")
                        for q in range(4):
                            tr(ps[bk][:, q * 128:(q + 1) * 128], sld[:, g4 * 4 + q, :], ["sld"], [("ps", bk)])
                        cp("act", hsb[:, g4 * 4:g4 * 4 + 4, :],
                           ps[bk][:, :].rearrange("p (k t) -> p k t", t=128), [("ps", bk)], HS[g4 * 4:g4 * 4 + 4])
                        cp("dve", hsbb[:, g4 * 4:g4 * 4 + 4, :],
                           ps[bk][:, :].rearrange("p (k t) -> p k t", t=128), [("ps", bk)], HSB[g4 * 4:g4 * 4 + 4])
                    run_gens([p2_chain(0, [0, 4], sb0), p2_chain(1, [1, 5], sb0),
                              p2_chain(2, [2, 6], sb0), p2_chain(3, [3, 7], sb0)])
                    for g4 in range(2):
                        bk = bank("tm")
                        for q in range(4):
                            tr(ps[bk][:, q * 128:(q + 1) * 128], hsb[:, g4 * 4 + q, :], [HS[g4 * 4 + q]], [("ps", bk)])
                        cp("act", sld[:, g4 * 4:g4 * 4 + 4, :],
                           ps[bk][:, :].rearrange("p (k t) -> p k t", t=128), [("ps", bk)], ["sld"])
                    for hh in range(2):
                        dma("pool", wkv_s[sb0:sb0 + 8, 2 * m + hh, :, :].rearrange("s i j -> i s j"),
                            sld[hh * 64:hh * 64 + 64, :, hh * 64:hh * 64 + 64], ["sld"], ())

            def post(m, par):
                yt, bon, gt = YT[par], BN[par], GT[par]
                q1, q2 = (1, 2) if par == 0 else (18, 19)
                y = T(yt)
                bk = bank("big")
                mm(ps[bk][:, 0:N], bones[:, :], y[:, 0:N], True, True, ["bones", tn(yt)], [("ps", bk)])
                stt(T(q1)[:, 0:N], ps[bk][:, 0:N], -1.0 / 64.0, y[:, 0:N], ALU.mult, ALU.add,
                    [("ps", bk), tn(yt)], [tn(q1)])
                tt("pool", T(q2)[:, 0:N], T(q1)[:, 0:N], T(q1)[:, 0:N], ALU.mult, [tn(q1)], [tn(q2)])
                yield
                bk = bank("big")
                mm(ps[bk][:, 0:N], bones[:, :], T(q2)[:, 0:N], True, True, ["bones", tn(q2)], [("ps", bk)])
                act(T(q2)[:, 0:N], ps[bk][:, 0:N], AF.Ln, [("ps", bk), "epsc"], [tn(q2)],
                    bias=col_eps_gn, scale=1.0 / 64.0)
                act(T(q2)[:, 0:N], T(q2)[:, 0:N], AF.Exp, [tn(q2)], [tn(q2)], scale=-0.5)
                yield
                tt("dve", T(q1)[:, 0:N], T(q1)[:, 0:N], T(q2)[:, 0:N], ALU.mult, [tn(q1), tn(q2)], [tn(q1)])
                ts("dve", T(q1)[:, 0:N], T(q1)[:, 0:N], col(C_LG + m), col(C_LB + m), ALU.mult, ALU.add,
                   [tn(q1), "cols"], [tn(q1)])
                tt("dve", T(q1)[:, 0:N], T(q1)[:, 0:N], T(bon)[:, 0:N], ALU.add, [tn(q1), tn(bon)], [tn(q1)])
                tt("dve", z_fm[:, m, 0:N], T(q1)[:, 0:N], T(gt)[:, 0:N], ALU.mult,
                   [tn(q1), tn(gt)], [("z", m)])

            zb_fm = reg1[:, 8:16, :]

            def sc_group_gen(pg):
                slots = [wjob(w_in, 0, 8, RW + which * 1024 + pg * 256, 256) for which in range(3)]
                yield
                for pm in range(2):
                    c = pg * 2 + pm
                    bh = proj_fm(slots[0], pm * 128, xn_fm, XN, N)
                    cp("act", T(25)[:, 0:N], ps[bh][:, 0:N], [("ps", bh)], ["t25"])
                    yield
                    bb = proj_fm(slots[1], pm * 128, xn_fm, XN, N)
                    cp("act", T(26)[:, 0:N], ps[bb][:, 0:N], [("ps", bb)], ["t26"])
                    yield
                    bc = proj_fm(slots[2], pm * 128, xn_fm, XN, N)
                    tt("dve", T(27)[:, 0:N], ps[bc][:, 0:N], T(25)[:, 0:N], ALU.mult, [("ps", bc), "t25"], ["t27"])
                    yield
                    conv3(ti, T(27), "t27", T(28), "t28", scc[:, c, :, :], ("scc", c),
                          (col(C_CS + 0 * 8 + c), col(C_CS + 1 * 8 + c), col(C_CS + 2 * 8 + c)))
                    tt("dve", zb_fm[:, c, 0:N], T(28)[:, 0:N], T(26)[:, 0:N], ALU.mult, ["t28", "t26"],
                       [("zb", c)])
                    yield

            regions = []
            c_ = 0
            for C in (chunks + [64] if has_s else chunks):
                if regions and regions[-1][2] == C:
                    regions[-1] = (regions[-1][0], regions[-1][1] + 1, C)
                else:
                    regions.append((c_, 1, C))
                c_ += C
            for pg in range(4):
                slots = [wjob(w_in, 0, 8, which * 1024 + pg * 256, 256) for which in range(3)]
                prep_ls(pg * 2, 0)
                prep_front(pg * 2, 0, slots)
                prep_front(pg * 2 + 1, 1, slots)
                prep_back(pg * 2, 0, 0)
                fill(0, regions, 0, False, IDS[0][5], IDS[0][6])
                prep_ls(pg * 2 + 1, 1)
                prep_back(pg * 2 + 1, 1, 1)
                fill(1, regions, 0, False, IDS[1][5], IDS[1][6])
                run_gens([prompt_gen(pg * 2 + pm, pm) for pm in range(2)] + [sc_group_gen(pg)])
                if has_s:
                    for pm in range(2):
                        sample_part(pg * 2 + pm, pm)
                run_gens([post(pg * 2 + pm, pm) for pm in range(2)])
            mg_fm = reg1[:, 16:24, :]
            ZN = [("z", m) for m in range(8)]
            ZBN = [("zb", m) for m in range(8)]
            def e_chunk(c, pm, s_a, s_b, s_ga, s_gb, t4):
                a1, a2, a3, a4 = t4
                bk = proj_fm(s_ga, pm * 128, xn_fm, XN, N, grp="wide")
                act(T(a1)[:, 0:N], ps[bk][:, 0:N], AF.Sigmoid, [("ps", bk), "cols"], [tn(a1)], bias=col(C_BG + c))
                yield
                bk = proj_fm(s_gb, pm * 128, xn_fm, XN, N, grp="wide")
                act(T(a2)[:, 0:N], ps[bk][:, 0:N], AF.Sigmoid, [("ps", bk), "cols"], [tn(a2)],
                    bias=col(C_BG + 8 + c))
                yield
                bk = proj_fm(s_a, pm * 128, z_fm, ZN, N, grp="wide")
                tt("dve", T(a3)[:, 0:N], ps[bk][:, 0:N], T(a1)[:, 0:N], ALU.mult, [("ps", bk), tn(a1)], [tn(a3)])
                yield
                bk = proj_fm(s_b, pm * 128, zb_fm, ZBN, N, grp="wide")
                tt("dve", T(a4)[:, 0:N], ps[bk][:, 0:N], T(a2)[:, 0:N], ALU.mult, [("ps", bk), tn(a2)], [tn(a4)])
                tt("dve", mg_fm[:, c, 0:N], T(a3)[:, 0:N], T(a4)[:, 0:N], ALU.add, [tn(a3), tn(a4)], [("mg", c)])
                yield

            TSETS = [(1, 2, 3, 4), (5, 6, 7, 8), (9, 10, 11, 12), (13, 14, 15, 16)]
            for pg in range(4):
                s_a = wjob(w_br, 0, 8, pg * 256, 256)
                s_b = wjob(w_bs, 0, 8, pg * 256, 256)
                s_ga = wjob(w_in, 0, 8, RW + 3072 + pg * 256, 256)
                s_gb = wjob(w_in, 0, 8, RW + 3072 + 1024 + pg * 256, 256)
                run_gens([e_chunk(pg * 2 + pm, pm, s_a, s_b, s_ga, s_gb, TSETS[pm]) for pm in range(2)])
            MGN = [("mg", m) for m in range(8)]
            for cg in range(4):
                slot = wjob(w_out, 0, 8, cg * 256, 256)
                for b in range(nblk):
                    bk = bank("tm")
                    for kc in range(8):
                        mm(ps[bk][:, 0:256], mg_fm[:, kc, b * 128:(b + 1) * 128],
                           wring[:, slot, kc, 0:256], kc == 0, kc == 7,
                           [("w", slot)] + MGN, [("ps", bk)])
                    tt("dve", x_tm[:, b, cg * 256:(cg + 1) * 256], x_tm[:, b, cg * 256:(cg + 1) * 256],
                       ps[bk][:, 0:256], ALU.add, [("ps", bk), ("x", b)], [("x", b)])
            run_gens([rms_gen(x_tm[:, b, :], [("x", b)], C_G2, b, xn_fm, ["xn_fm"]) for b in range(nblk)])
            h_fm = reg1[:, 0:22, :]
            def g_chunk(c, pm, s_g, s_v, t4):
                a1, a2, a3, a4 = t4
                bk = proj_fm(s_g, pm * 128, xn_fm, XN, N, grp="wide")
                cp("act", T(a1)[:, 0:N], ps[bk][:, 0:N], [("ps", bk)], [tn(a1)])
                act(T(a2)[:, 0:N], ps[bk][:, 0:N], AF.Identity, [("ps", bk), "cols"], [tn(a2)],
                    scale=col(C_CF + 2 * 44 + c))
                yield
                conv3(ti, T(a1), tn(a1), T(a2), tn(a2), ffc[:, c, :, :], ("ffc", c),
                      (col(C_CF + 0 * 44 + c), col(C_CF + 1 * 44 + c), col(C_CF + 2 * 44 + c)), skip_first=True)
                act(T(a2)[:, 0:N], T(a2)[:, 0:N], AF.Silu, [tn(a2)], [tn(a2)])
                yield
                bk = proj_fm(s_v, pm * 128, xn_fm, XN, N, grp="wide")
                cp("act", T(a3)[:, 0:N], ps[bk][:, 0:N], [("ps", bk)], [tn(a3)])
                cv = 22 + c
                act(T(a4)[:, 0:N], ps[bk][:, 0:N], AF.Identity, [("ps", bk), "cols"], [tn(a4)],
                    scale=col(C_CF + 2 * 44 + cv))
                yield
                conv3(ti, T(a3), tn(a3), T(a4), tn(a4), ffc[:, cv, :, :], ("ffc", cv),
                      (col(C_CF + 0 * 44 + cv), col(C_CF + 1 * 44 + cv), col(C_CF + 2 * 44 + cv)), skip_first=True)
                tt("pool", h_fm[:, c, 0:N], T(a2)[:, 0:N], T(a4)[:, 0:N], ALU.mult, [tn(a2), tn(a4)], [("h", c)])
                yield

            for g0 in range(0, 11, 2):
                gens = []
                for gi, g in enumerate(range(g0, min(g0 + 2, 11))):
                    s_g = wjob(w_up, 0, 8, g * 256, 256)
                    s_v = wjob(w_up, 0, 8, DFF + g * 256, 256)
                    for pm in range(2):
                        gens.append(g_chunk(g * 2 + pm, pm, s_g, s_v, TSETS[gi * 2 + pm]))
                run_gens(gens)
            HN = [("h", c) for c in range(22)]
            for cg in range(4):
                kparts = [(0, 8), (8, 8), (16, 6)]
                slots = [wjob(w_down, k0 * 128, nk, cg * 256, 256) for (k0, nk) in kparts]
                for b in range(nblk):
                    bk = bank("tm")
                    for pi, (k0, nk) in enumerate(kparts):
                        for kk_ in range(nk):
                            kc = k0 + kk_
                            mm(ps[bk][:, 0:256], h_fm[:, kc, b * 128:(b + 1) * 128],
                               wring[:, slots[pi], kk_, 0:256], kc == 0, kc == 21,
                               [("w", slots[pi])] + HN, [("ps", bk)])
                    tt("dve", x_tm[:, b, cg * 256:(cg + 1) * 256], x_tm[:, b, cg * 256:(cg + 1) * 256],
                       ps[bk][:, 0:256], ALU.add, [("ps", bk), ("x", b)], [("x", b)])
            run_gens([fin_gen(b, store_pieces[b]) for b in range(nblk)])

        for g4 in range(2):
            bk = bank("tm")
            for q in range(4):
                tr(ps[bk][:, q * 128:(q + 1) * 128], hblk[:, g4 * 4 + q, :], [("H", g4 * 4 + q)], [("ps", bk)])
            cp("act", sld[:, g4 * 4:g4 * 4 + 4, :], ps[bk][:, :].rearrange("p (k t) -> p k t", t=128),
               [("ps", bk)], ["sld"])
        wv = wkv_p.rearrange("(m two) i j -> two i m j", two=2)
        for hh in range(2):
            dma("pool", wv[hh], sld[hh * 64:hh * 64 + 64, 0:8, hh * 64:hh * 64 + 64], ["sld"], ())
        for c in range(27):
            w = 128 if c < 26 else 32
            b = bank("tm")
            tr(ps[b][0:17, 0:w], shc[0:w, c, :], [("shc", c)], [("ps", b)])
            cp("act", tm_scr[0:17, (c % 8) * 128:(c % 8) * 128 + w], ps[b][0:17, 0:w], [("ps", b)], ["tm_scr"])
            if c % 8 == 7 or c == 26:
                c0_ = (c // 8) * 8 * 128
                wp = min(1024, RW - c0_)
                dma("pool", shift_o[:, c0_:c0_ + wp], tm_scr[0:17, 0:wp], ["tm_scr"], ())
        for c in range(8):
            b = bank("tm")
            tr(ps[b][0:34, 0:128], scc[:, c, :, :].rearrange("p s r -> p (s r)"), [("scc", c)], [("ps", b)])
            cp("act", hsb[0:34, c, :], ps[b][0:34, 0:128], [("ps", b)], ["hsb"])
        dma("pool", sc_o.rearrange("r (c f) -> r c f", f=128), hsb[0:34, 0:8, :], ["hsb"], ())
        for c in range(44):
            b = bank("tm")
            tr(ps[b][0:34, 0:128], ffc[:, c, :, :].rearrange("p s r -> p (s r)"), [("ffc", c)], [("ps", b)])
            cp("act", tm_scr[0:34, (c % 8) * 128:(c % 8 + 1) * 128], ps[b][0:34, 0:128], [("ps", b)], ["tm_scr"])
            if c % 8 == 7 or c == 43:
                c0_ = (c // 8) * 8 * 128
                wp = min(1024, 2 * DFF - c0_)
                dma("pool", ffn_o[:, c0_:c0_ + wp], tm_scr[0:34, 0:wp], ["tm_scr"], ())

        final_waits = [(("d", k), S.dval[k]) for k in range(Sched.NDS) if S.dval[k] > 0]

        cum = {}
        for en in Sched.ENG:
            c = 0
            for fn, waits, me in S.streams[en]:
                if (not me.dma) and (me.semkey, me.val) in S.flag:
                    c += 1
                    cum[(me.semkey, me.val)] = c

        def emit(name, e):
            for fn, waits, me in S.streams[name]:
                for semkey, val, isdma in waits:
                    e.wait_ge(sems[semkey], val if isdma else cum[(semkey, val)])
                ins = fn(e)
                if me.dma:
                    ins.then_inc(sems[me.semkey], 16)
                elif (me.semkey, me.val) in S.flag:
                    ins.then_inc(sems[me.semkey], 1)

        with nc.Block() as block:
            @block.tensor
            def _(e):
                emit("pe", e)

            @block.scalar
            def _(e):
                emit("act", e)

            @block.vector
            def _(e):
                emit("dve", e)

            @block.gpsimd
            def _(e):
                emit("pool", e)

            @block.sync
            def _(e):
                emit("sp", e)
                for semkey, val in final_waits:
                    e.wait_ge(sems[semkey], val)
    return nc, recorded


def _consts():
    ident = np.eye(128, dtype=np.float32)
    bones = np.zeros((128, 128), np.float32)
    bones[:64, :64] = 1.0
    bones[64:, 64:] = 1.0
    s = np.arange(64)
    su = (s[:, None] < s[None, :]).astype(np.float32)
    sl = (s[:, None] > s[None, :]).astype(np.float32)
    incl = (s[:, None] <= s[None, :]).astype(np.float32)
    def bd(a):
        o = np.zeros((128, 128), np.float32)
        o[:64, :64] = a
        o[64:, 64:] = a
        return o
    maska = np.concatenate([bd(su), bd(sl), bd(su), np.concatenate([incl, incl], 0),
                            np.concatenate([incl, incl], 0)], axis=1)
    rmask = np.ones((128, 3, NW), np.float32)
    def starts(chunks, has_s):
        st, c = [], 0
        for C in chunks:
            st.append(c); c += C
        if has_s:
            for sg in range(NSEG):
                st.append(c + sg * SL)
        return st
    rmask[:, 0, starts(TILES[0][1], False)] = 0.0
    rmask[:, 1, starts(TILES[1][1], False)] = 0.0
    rmask[:, 2, starts(TILES[5][1], True)] = 0.0
    sg = s // SL
    same = (sg[:, None] == sg[None, :])
    su_s, sl_s, incl_s = su * same, sl * same, incl * same
    maskas = np.concatenate([bd(su_s), bd(sl_s), bd(su_s), np.concatenate([incl_s, incl_s], 0),
                             np.concatenate([incl_s, incl_s], 0)], axis=1).astype(np.float32)
    segmask = np.zeros((128, NSEG), np.float32)
    for g in range(NSEG):
        for hh in range(2):
            segmask[hh * 64 + g * SL:hh * 64 + (g + 1) * SL, g] = 1.0
    return ident, bones, np.ascontiguousarray(maska), rmask, np.ascontiguousarray(maskas), segmask


_NC_CACHE = {}


def kernel(**inp):
    f = lambda k: np.ascontiguousarray(np.asarray(inp[k], dtype=np.float32))
    x_prompt, x_sample = f("x_prompt"), f("x_sample")
    ident, bones, maska, rmask, maskas, segmask = _consts()

    def colfm(v, nchunk):
        v = np.asarray(v, np.float32).reshape(-1)
        pad = nchunk * 128 - v.shape[0]
        if pad:
            v = np.concatenate([v, np.zeros(pad, np.float32)])
        return v.reshape(nchunk, 128).T

    cols = np.zeros((128, NCOL), np.float32)
    mu = f("mu_shift")[0]
    cols[:, C_MU:C_MU + 27] = colfm(mu, 27)
    cols[:, C_W0:C_W0 + 8] = colfm(f("w0")[0], 8)
    cols[:, C_A0:C_A0 + 8] = colfm(f("a0")[0], 8)
    cols[:, C_KK:C_KK + 8] = colfm(f("k_k")[0], 8)
    cols[:, C_KA:C_KA + 8] = colfm(f("k_a")[0], 8)
    cols[:, C_RK:C_RK + 8] = colfm(f("r_k")[0], 8)
    cols[:, C_LG:C_LG + 8] = colfm(f("lnx_g")[0], 8)
    cols[:, C_LB:C_LB + 8] = colfm(f("lnx_b")[0], 8)
    cols[:, C_BG:C_BG + 16] = colfm(f("b_gate")[0], 16)
    cs = f("conv_sc")[0]
    for tap in range(3):
        cols[:, C_CS + tap * 8:C_CS + tap * 8 + 8] = colfm(cs[tap], 8)
    cf = f("conv_ffn")[0]
    for tap in range(3):
        cols[:, C_CF + tap * 44:C_CF + tap * 44 + 44] = colfm(cf[tap], 44)
    cols[:, C_G1:C_G1 + 8] = colfm(f("norm1_g")[0], 8)
    cols[:, C_G2:C_G2 + 8] = colfm(f("norm2_g")[0], 8)
    gbc = np.ascontiguousarray(np.broadcast_to(f("final_norm_g"), (128, D)), dtype=np.float32)
    lora_da = np.concatenate([f("w_decay_up")[0], f("w_aaa_up")[0]], axis=0)
    wg = f("w_gate_up")[0]
    lora_pk = np.zeros((8, 128, 3, 128), np.float32)
    for m_ in range(8):
        lora_pk[m_, :, 0, :] = lora_da[:, m_ * 128:(m_ + 1) * 128]
        lora_pk[m_, :, 1, :] = wg[0:128, m_ * 128:(m_ + 1) * 128]
        lora_pk[m_, 0:32, 2, :] = wg[128:160, m_ * 128:(m_ + 1) * 128]

    shared = {
        "meta": f("meta_tokens"), "w_in": f("w_in")[0], "w_br": f("w_branch_rwkv")[0],
        "w_bs": f("w_branch_sc")[0], "w_out": f("w_out")[0], "w_up": f("w_up")[0], "w_down": f("w_down")[0],
        "lora_pk": lora_pk, "cols": cols, "gbc": gbc, "ident": ident, "bones": bones,
        "maska": maska, "rmask": rmask, "maskas": maskas, "segmask": segmask,
    }
    st_wkv, st_shift = f("state_wkv")[0], f("state_shift")[0]
    st_sc, st_ffn = f("state_sc_conv")[0], f("state_ffn_conv")[0]
    in_maps = []
    for c in range(8):
        sl_ = slice(16 * c, 16 * c + 16)
        m = dict(shared)
        m["xp"] = x_prompt[c]
        m["xs"] = np.ascontiguousarray(x_sample[sl_].reshape(64, D))
        m["st_wkv"] = np.ascontiguousarray(st_wkv[sl_])
        m["st_shift"] = np.ascontiguousarray(st_shift[sl_])
        m["st_sc"] = np.ascontiguousarray(st_sc[sl_].reshape(32, D))
        m["st_ffn"] = np.ascontiguousarray(st_ffn[sl_].reshape(32, 2 * DFF))
        in_maps.append(m)

    if "nc" not in _NC_CACHE:
        _, order = build_program(None)
        assert len(order) == 480 and all(order[j] == order[j % 80] for j in range(480))
        _NC_CACHE["nc"], order2 = build_program(order)
        assert order == order2
    nc = _NC_CACHE["nc"]
    res = run_bass_kernel_spmd(nc, in_maps, core_ids=list(range(8)))
    R = res.results
    y_prompt = np.stack([R[c]["y_p"] for c in range(8)], 0)
    y_sample = np.concatenate([R[c]["y_s"].reshape(16, 4, D) for c in range(8)], 0)
    wkv_p = np.stack([R[c]["wkv_p"] for c in range(8)], 0)[None]
    wkv_s = np.concatenate([R[c]["wkv_s"] for c in range(8)], 0)[None]
    shift_p = np.stack([R[c]["shift_o"][16] for c in range(8)], 0)[None]
    shift_s = np.concatenate([R[c]["shift_o"][0:16] for c in range(8)], 0)[None]
    sc_p = np.stack([R[c]["sc_o"][32:34] for c in range(8)], 0)[None]
    sc_s = np.concatenate([R[c]["sc_o"][0:32].reshape(16, 2, D) for c in range(8)], 0)[None]
    ffn_p = np.stack([R[c]["ffn_o"][32:34] for c in range(8)], 0)[None]
    ffn_s = np.concatenate([R[c]["ffn_o"][0:32].reshape(16, 2, 2 * DFF) for c in range(8)], 0)[None]
    outs = (y_prompt, y_sample, wkv_p, wkv_s, shift_p, shift_s, sc_p, sc_s, ffn_p, ffn_s)
    return tuple(np.ascontiguousarray(o, dtype=np.float32) for o in outs)
```

```python
import numpy as np
import concourse.bass as bass
import concourse.mybir as mybir
from concourse.bass_utils import run_bass_kernel_spmd

F32 = mybir.dt.float32
F32R = mybir.dt.float32r
BF16 = mybir.dt.bfloat16
AF = mybir.ActivationFunctionType
ALU = mybir.AluOpType
AX = mybir.AxisListType

D = 1024
NH = 16
DFF = 2816
RW = 3360
PT = 8480
SEQ = 2048
NMETA = 16
NPOS = SEQ + NMETA
NW = 384
EXPM05 = 0.6065306597126334
RMS_EPS = 1e-6
GN_EPS = 64e-5

TILES = [
    (0, [16, 64, 64, 64, 64, 64], False),
    (336, [64] * 6, False),
    (720, [64] * 6, False),
    (1104, [64] * 6, False),
    (1488, [64] * 6, False),
    (1872, [64] * 3, True),
]
NSEG = 16
SL = 4

C_MU = 0
C_W0 = 27
C_A0 = 35
C_KK = 43
C_KA = 51
C_RK = 59
C_LG = 67
C_LB = 75
C_BG = 83
C_CS = 99
C_CF = 123
C_OMKA = 255
C_G1 = 264
C_G2 = 272
NCOL = 280


class Op:
    __slots__ = ("eng", "semkey", "val", "dma")

    def __init__(self, eng, semkey, val, dma):
        self.eng, self.semkey, self.val, self.dma = eng, semkey, val, dma


class Sched:
    ENG = ["pe", "act", "dve", "pool", "sp"]
    NDS = 20
    POOLS = {"sp": (0, 12), "pool": (12, 20), "act": (12, 20)}

    def __init__(self):
        self.streams = {e: [] for e in self.ENG}
        self.cnt = {e: 0 for e in self.ENG}
        self.lastw = {}
        self.readers = {}
        self.waited = {}
        self.rr = {"sp": 0, "pool": 0, "act": 0}
        self.dval = [0] * self.NDS
        self.flag = set()

    def op(self, eng, fn, reads=(), writes=(), dma=False):
        psr = [n for n in reads if isinstance(n, tuple) and n[0] == "ps"]
        if psr:
            reads = [n for n in reads if n not in psr]
            writes = list(writes) + [n for n in psr if n not in writes]
        deps = []
        raw = set()
        for n in reads:
            w = self.lastw.get(n)
            if w is not None:
                deps.append(w)
                raw.add(id(w))
        for n in writes:
            w = self.lastw.get(n)
            if w is not None:
                deps.append(w)
            deps.extend(self.readers.get(n, ()))
        if dma:
            lo, hi = self.POOLS[eng]
            k = lo + self.rr[eng]
            self.rr[eng] = (self.rr[eng] + 1) % (hi - lo)
            semkey = ("d", k)
            if self.dval[k] > 0:
                deps.append(Op(None, semkey, self.dval[k], True))
            self.dval[k] += 16
            me = Op(eng, semkey, self.dval[k], True)
        else:
            self.cnt[eng] += 1
            me = Op(eng, ("c", eng), self.cnt[eng], False)
        waits = []
        for d in deps:
            if (not d.dma) and d.eng == eng and eng == "pe":
                continue
            key = (eng, d.semkey)
            if self.waited.get(key, 0) >= d.val:
                continue
            self.waited[key] = d.val
            waits.append((d.semkey, d.val, d.dma))
            if not d.dma:
                self.flag.add((d.semkey, d.val))
        self.streams[eng].append((fn, waits, me))
        for n in reads:
            self.readers.setdefault(n, []).append(me)
        for n in writes:
            self.lastw[n] = me
            self.readers[n] = []
        return me


def build_program(job_order=None):
    nc = bass.Bass("TRN2", target_bir_lowering=False)
    nc.dge_precook = False
    S = Sched()

    def din(name, shape, dt=F32):
        return nc.dram_tensor(name, list(shape), dt, kind="ExternalInput").ap()

    def dout(name, shape):
        return nc.dram_tensor(name, list(shape), F32, kind="ExternalOutput").ap()

    xp = din("xp", [SEQ, D])
    xs_in = din("xs", [64, D])
    meta = din("meta", [NMETA, D])
    st_wkv = din("st_wkv", [NSEG, NH, 64, 64])
    st_shift = din("st_shift", [NSEG, RW])
    st_sc = din("st_sc", [2 * NSEG, D])
    st_ffn = din("st_ffn", [2 * NSEG, 2 * DFF])
    w_in = din("w_in", [D, PT])
    w_br = din("w_br", [D, D])
    w_bs = din("w_bs", [D, D])
    w_out = din("w_out", [D, D])
    w_up = din("w_up", [D, 2 * DFF])
    w_down = din("w_down", [DFF, D])
    lora_pk = din("lora_pk", [8, 128, 3, 128])
    cols_in = din("cols", [128, NCOL])
    gbc_in = din("gbc", [128, D])
    ident_in = din("ident", [128, 128])
    bones_in = din("bones", [128, 128])
    maska_in = din("maska", [128, 512])
    rmask_in = din("rmask", [128, 3, NW])
    maskas_in = din("maskas", [128, 512])
    segmask_in = din("segmask", [128, NSEG])

    NJT = 80
    wbf = nc.dram_tensor("wbf", [NJT, 128, 8 * 256], BF16, kind="Internal").ap()
    y_p = dout("y_p", [SEQ, D])
    y_s = dout("y_s", [64, D])
    wkv_p = dout("wkv_p", [NH, 64, 64])
    wkv_s = dout("wkv_s", [NSEG, NH, 64, 64])
    shift_o = dout("shift_o", [17, RW])
    sc_o = dout("sc_o", [34, D])
    ffn_o = dout("ffn_o", [34, 2 * DFF])

    import contextlib
    es = contextlib.ExitStack()
    with es:
        def sb(name, shape, dt=F32):
            return es.enter_context(nc.sbuf_tensor(name, list(shape), dt))

        x_tm = sb("x_tm", [128, 3, D])
        tm_scr = sb("tm_scr", [128, D])
        xn_fm = sb("xn_fm", [128, 8, NW], BF16)
        lor_x = sb("lor_x", [128, 3, NW])
        reg1 = sb("reg1", [128, 24, NW], BF16)
        NR = 7
        NST = 3
        wst = sb("wst", [128, NST, 8, 256])
        wring = sb("wring", [128, NR, 8, 256], BF16)
        NTMP = 29
        tmps = sb("tmps", [128, NTMP, NW])
        lor_p = tmps[:, 18:21, :]
        blk = sb("blk", [128, 2, 4, 6, 128], BF16)
        amat = sb("amat", [128, 2, 512], BF16)
        msb = sb("msb", [128, 2, 2, 128], BF16)
        pq = sb("pq", [128, 2, 2, 256], BF16)
        tmsb = sb("tmsb", [128, 2, 384], BF16)
        xsb = sb("xsb", [128, 4, 128], BF16)
        usb = sb("usb", [128, 4, 128], BF16)
        hblk = sb("hblk", [128, 8, 128])
        hbb = sb("hbb", [128, 8, 128], BF16)
        hsbb = sb("hsbb", [128, 8, 128], BF16)
        htmp = sb("htmp", [128, 4, 128])
        rtb = sb("rtb", [128, 2, NW], BF16)
        identb = sb("identb", [128, 128], BF16)
        sld = sb("sld", [128, 8, 128])
        hsb = sb("hsb", [128, 8, 128])
        shc = sb("shc", [128, 27, 17])
        scc = sb("scc", [128, 8, 17, 2])
        ffc = sb("ffc", [128, 44, 17, 2])
        lwp = sb("lwp", [128, 2, 3, 128])
        cols = sb("cols_sb", [128, NCOL])
        gbc = sb("gbc_sb", [128, D])
        ident = sb("ident_sb", [128, 128])
        bones = sb("bones_sb", [128, 128])
        maska = sb("maska_sb", [128, 512])
        rmask = sb("rmask_sb", [128, 3, NW])
        maskas = sb("maskas_sb", [128, 512])
        segmask = sb("segmask_sb", [128, NSEG])
        vgs = sb("vgs", [128, 4, 128], BF16)
        stat = sb("stat", [128, 16])
        stld = tm_scr

        ps = [es.enter_context(nc.psum_tensor("ps%d" % i, [128, 512], F32)) for i in range(8)]

        sems = {}
        for e in Sched.ENG:
            sems[("c", e)] = es.enter_context(nc.semaphore("c_" + e))
        for k in range(Sched.NDS):
            sems[("d", k)] = es.enter_context(nc.semaphore("d_%d" % k))

        def dma(q, out, in_, reads, writes):
            return S.op(q, lambda e: e.dma_start(out=out, in_=in_), reads, writes, dma=True)

        def mm(out, lhsT, rhs, start, stop, reads, writes):
            return S.op("pe", lambda e: e.matmul(out, lhsT, rhs, start=start, stop=stop), reads, writes)

        def tr(out, in_, reads, writes):
            P_ = in_.shape[0]
            idt = identb if in_.dtype == BF16 else ident
            return S.op("pe", lambda e: e.transpose(out, in_, idt[0:P_, 0:P_]), list(reads) + ["ident", "identb"], writes)

        def act(out, in_, func, reads, writes, bias=None, scale=None):
            kw = {}
            if bias is not None:
                kw["bias"] = bias
            if scale is not None:
                kw["scale"] = scale
            return S.op("act", lambda e: e.activation(out, in_, func, **kw), reads, writes)

        def tt(eng, out, in0, in1, op, reads, writes):
            return S.op(eng, lambda e: e.tensor_tensor(out, in0, in1, op), reads, writes)

        def ts(eng, out, in0, s1, s2, op0, op1, reads, writes):
            if s2 is None:
                return S.op(eng, lambda e: e.tensor_scalar(out, in0, s1, None, op0), reads, writes)
            return S.op(eng, lambda e: e.tensor_scalar(out, in0, s1, s2, op0, op1), reads, writes)

        def stt(out, in0, sc, in1, op0, op1, reads, writes):
            return S.op("dve", lambda e: e.scalar_tensor_tensor(out, in0, sc, in1, op0, op1), reads, writes)

        def cp(eng, out, in_, reads, writes):
            if eng == "act":
                return act(out, in_, AF.Copy, reads, writes)
            return S.op(eng, lambda e: e.tensor_copy(out, in_), reads, writes)

        def memset(eng, ap, val, writes):
            return S.op(eng, lambda e: e.memset(ap, val), (), writes)

        def col(i):
            return cols[:, i:i + 1]

        rot = {"big": [0, 1], "tm": [2, 3], "scan0": [4, 5], "scan1": [6, 7], "wide": [0, 1, 4, 5, 6, 7]}
        rpos = {"big": 0, "tm": 0, "scan0": 0, "scan1": 0, "wide": 0}

        def bank(group):
            b = rot[group][rpos[group] % len(rot[group])]
            rpos[group] += 1
            return b

        wj = {"n": 0, "iss": 0}

        WTS = {"w_in": w_in, "w_br": w_br, "w_bs": w_bs, "w_out": w_out, "w_up": w_up, "w_down": w_down}
        WNAME = {id(v): k for k, v in WTS.items()}
        recorded = []
        JOBS = [(WTS[n], a, b, c, d) for (n, a, b, c, d) in job_order] if job_order is not None else None
        LA = 3

        def issue(j, jb=None):
            W, k0, nk, c0, ncols = JOBS[j] if jb is None else jb
            st = j % NST
            slot = j % NR
            jj = j % NJT
            scr = wbf[jj, :, 0:nk * ncols].rearrange("p (k c) -> p k c", c=ncols)
            if j < NJT:
                src = W[k0:k0 + nk * 128, c0:c0 + ncols].rearrange("(k p) c -> p k c", p=128)
                dma("sp", wst[:, st, 0:nk, 0:ncols], src, (), [("ws", st)])
                cp("act", wring[:, slot, 0:nk, 0:ncols], wst[:, st, 0:nk, 0:ncols], [("ws", st)], [("w", slot)])
                dma("pool", scr, wring[:, slot, 0:nk, 0:ncols], [("w", slot)], [("wbf", jj)])
            else:
                dma("sp", wring[:, slot, 0:nk, 0:ncols], scr, [("wbf", jj)], [("w", slot)])

        def wjob(W, k0, nk, c0, ncols):
            j = wj["n"]
            recorded.append((WNAME[id(W)], k0, nk, c0, ncols))
            if JOBS is None:
                issue(j, (W, k0, nk, c0, ncols))
            else:
                jb = JOBS[j]
                assert jb[0] is W and jb[1:] == (k0, nk, c0, ncols), (j, jb[1:], (k0, nk, c0, ncols))
                while wj["iss"] < min(len(JOBS), j + 1 + LA):
                    issue(wj["iss"])
                    wj["iss"] += 1
            wj["n"] += 1
            return j % NR

        tmp_free = list(range(NTMP))

        def T(i):
            return tmps[:, i, :]

        dma("sp", cols[:, :], cols_in, (), ["cols"])
        dma("sp", gbc[:, :], gbc_in, (), ["gbc"])
        dma("sp", ident[:, :], ident_in, (), ["ident"])
        dma("sp", bones[:, :], bones_in, (), ["bones"])
        dma("sp", maska[:, :], maska_in, (), ["maska"])
        dma("sp", rmask[:, :, :], rmask_in, (), ["rmask"])
        dma("sp", maskas[:, :], maskas_in, (), ["maskas"])
        dma("sp", segmask[:, :], segmask_in, (), ["segmask"])
        ts("dve", cols[:, C_OMKA:C_OMKA + 8], cols[:, C_KA:C_KA + 8], -1.0, 1.0, ALU.mult, ALU.add,
           ["cols"], ["cols"])
        memset("dve", blk[:, :, :, :, :], 0.0, [("blk", 0), ("blk", 1)])
        memset("dve", hblk[:, :, :], 0.0, [("H", m) for m in range(8)])
        memset("dve", hbb[:, :, :], 0.0, [("Hb", m) for m in range(8)])
        cp("dve", identb[:, :], ident[:, :], ["ident"], ["identb"])
        memset("dve", tmps[:, 0, :], 0.0, ["t0"])
        memset("dve", reg1[:, :, :], 0.0, ["reg1"])
        memset("dve", sld[:, :, :], 0.0, ["sld"])
        memset("dve", xn_fm[:, :, :], 0.0, ["xn_fm"])
        memset("dve", shc[:, :, :], 0.0, [("shc", c) for c in range(27)])
        memset("dve", scc[:, :, :, :], 0.0, [("scc", c) for c in range(8)])
        memset("dve", ffc[:, :, :, :], 0.0, [("ffc", c) for c in range(44)])
        memset("dve", x_tm[:, :, :], 0.0, [("x", b) for b in range(3)])

        for c in range(27):
            w = 128 if c < 26 else 32
            if c % 8 == 0:
                wp = min(1024, RW - c * 128)
                dma("sp", tm_scr[0:16, 0:wp], st_shift[:, c * 128:c * 128 + wp], (), ["tm_scr"])
            b = bank("tm")
            tr(ps[b][0:w, 0:16], tm_scr[0:16, (c % 8) * 128:(c % 8) * 128 + w], ["tm_scr"], [("ps", b)])
            cp("act", shc[0:w, c, 0:16], ps[b][0:w, 0:16], [("ps", b)], [("shc", c)])
        dma("sp", tm_scr[0:32, 0:D], st_sc, (), ["tm_scr"])
        for c in range(8):
            b = bank("tm")
            tr(ps[b][:, 0:32], tm_scr[0:32, c * 128:(c + 1) * 128], ["tm_scr"], [("ps", b)])
            cp("act", scc[:, c, 0:16, :], ps[b][:, 0:32].rearrange("p (s r) -> p s r", r=2),
               [("ps", b)], [("scc", c)])
        for c in range(44):
            if c % 8 == 0:
                wp = min(1024, 2 * DFF - c * 128)
                dma("sp", tm_scr[0:32, 0:wp], st_ffn[:, c * 128:c * 128 + wp], (), ["tm_scr"])
            b = bank("tm")
            tr(ps[b][:, 0:32], tm_scr[0:32, (c % 8) * 128:(c % 8 + 1) * 128], ["tm_scr"], [("ps", b)])
            cp("act", ffc[:, c, 0:16, :], ps[b][:, 0:32].rearrange("p (s r) -> p s r", r=2),
               [("ps", b)], [("ffc", c)])

        def load_x_block(ti, b):
            pos0, chunks, has_s = TILES[ti]
            npr = sum(chunks)
            c0 = b * 128
            pieces = []
            r = 0
            while r < 128:
                cpos = c0 + r
                if cpos < npr:
                    p = pos0 + cpos
                    if p < NMETA:
                        n = min(NMETA - p, 128 - r, npr - cpos)
                        pieces.append((r, n, "meta", p))
                    else:
                        n = min(128 - r, npr - cpos)
                        pieces.append((r, n, "xp", p - NMETA))
                    r += n
                elif has_s and cpos < npr + 64:
                    n = min(128 - r, npr + 64 - cpos)
                    pieces.append((r, n, "xs", cpos - npr))
                    r += n
                else:
                    break
            for (r0, n, kind, sr) in pieces:
                src = {"meta": meta, "xp": xp, "xs": xs_in}[kind]
                dma("sp", x_tm[r0:r0 + n, b, :], src[sr:sr + n, :], (), [("x", b)])
            return pieces

        def rms_to_fm(src_ap, src_names, gc0, b, dst_fm, dst_names):
            tt("dve", tm_scr[:, :], src_ap, src_ap, ALU.mult, src_names, ["tm_scr"])
            S.op("dve", lambda e: e.reduce_sum(stat[:, 0:1], tm_scr[:, :], AX.X), ["tm_scr"], ["stat0"])
            act(stat[:, 1:2], stat[:, 0:1], AF.Sqrt, ["stat0", "epsc"], ["stat1"], bias=col_eps_rms, scale=1.0 / D)
            S.op("dve", lambda e: e.reciprocal(stat[:, 2:3], stat[:, 1:2]), ["stat1"], ["stat2"])
            ts("dve", tm_scr[:, :], src_ap, stat[:, 2:3], None, ALU.mult, None,
               list(src_names) + ["stat2"], ["tm_scr"])
            for half in range(2):
                bk = bank("tm")
                for q in range(4):
                    kc = half * 4 + q
                    tr(ps[bk][:, q * 128:(q + 1) * 128], tm_scr[:, kc * 128:(kc + 1) * 128],
                       ["tm_scr"], [("ps", bk)])
                for q in range(4):
                    kc = half * 4 + q
                    act(dst_fm[:, kc, b * 128:(b + 1) * 128], ps[bk][:, q * 128:(q + 1) * 128],
                        AF.Identity, [("ps", bk), "cols"], dst_names, scale=col(gc0 + kc))

        def nscr(b):
            return tmps[:, 3 * b:3 * b + 3, :].rearrange("p a c -> p (a c)")[:, 0:D]

        def nscr_names(b):
            return ["t%d" % i for i in range(3 * b, 3 * b + 3)]

        def rms_gen(src_ap, src_names, gc0, b, dst_fm, dst_names):
            scr, SN = nscr(b), nscr_names(b)
            s0, s1, s2 = (stat[:, 3 * b + i:3 * b + i + 1] for i in range(3))
            n0, n1, n2 = (("stat", b, i) for i in range(3))
            tt("dve", scr, src_ap, src_ap, ALU.mult, src_names, SN)
            S.op("dve", lambda e: e.reduce_sum(s0, scr, AX.X), SN, [n0])
            yield
            act(s1, s0, AF.Sqrt, [n0, "epsc"], [n1], bias=col_eps_rms, scale=1.0 / D)
            yield
            S.op("dve", lambda e: e.reciprocal(s2, s1), [n1], [n2])
            ts("dve", scr, src_ap, s2, None, ALU.mult, None, list(src_names) + [n2], SN)
            yield
            for half in range(2):
                bk = bank("wide")
                for q in range(4):
                    kc = half * 4 + q
                    tr(ps[bk][:, q * 128:(q + 1) * 128], scr[:, kc * 128:(kc + 1) * 128], SN, [("ps", bk)])
                for q in range(4):
                    kc = half * 4 + q
                    act(dst_fm[:, kc, b * 128:(b + 1) * 128], ps[bk][:, q * 128:(q + 1) * 128],
                        AF.Identity, [("ps", bk), "cols"], dst_names, scale=col(gc0 + kc))
                yield

        def fin_gen(b, pieces):
            xb = x_tm[:, b, :]
            scr, SN = nscr(b), nscr_names(b)
            s0, s1, s2 = (stat[:, 3 * b + i:3 * b + i + 1] for i in range(3))
            n0, n1, n2 = (("stat", b, i) for i in range(3))
            tt("dve", scr, xb, xb, ALU.mult, [("x", b)], SN)
            S.op("dve", lambda e: e.reduce_sum(s0, scr, AX.X), SN, [n0])
            yield
            act(s1, s0, AF.Sqrt, [n0, "epsc"], [n1], bias=col_eps_rms, scale=1.0 / D)
            yield
            S.op("dve", lambda e: e.reciprocal(s2, s1), [n1], [n2])
            stt(xb, xb, s2, gbc[:, :], ALU.mult, ALU.mult, [("x", b), n2, "gbc"], [("x", b)])
            for (r0, n, kind, sr) in pieces:
                if kind == "meta":
                    continue
                dst = y_p if kind == "xp" else y_s
                dma("pool", dst[sr:sr + n, :], x_tm[r0:r0 + n, b, :], [("x", b)], ())
            yield

        epsc = sb("epsc", [128, 4])
        memset("dve", epsc[:, 0:1], RMS_EPS, ["epsc"])
        memset("dve", epsc[:, 1:2], GN_EPS, ["epsc"])
        memset("dve", epsc[:, 2:3], 1e-18, ["epsc"])
        col_eps_rms = epsc[:, 0:1]
        col_eps_gn = epsc[:, 1:2]

        def proj_fm(slot, off, xin, xin_names, N, nk=8, grp="big"):
            bk = bank(grp)
            for kc in range(nk):
                mm(ps[bk][:, 0:N], wring[:, slot, kc, off:off + 128],
                   xin[:, kc, 0:N], kc == 0, kc == nk - 1,
                   [("w", slot)] + list(xin_names), [("ps", bk)])
            return bk

        def token_shift(ti, p_ap, p_name, d_ap, d_name, out_ap, out_name, ci, nrows=128):
            pos0, chunks, has_s = TILES[ti]
            npr = sum(chunks)
            N = npr + (64 if has_s else 0)
            R = slice(0, nrows)
            tt("dve", d_ap[R, 1:npr], p_ap[R, 0:npr - 1], p_ap[R, 1:npr], ALU.subtract, [p_name], [d_name])
            tt("pool", d_ap[R, 0:1], shc[R, ci, 16:17], p_ap[R, 0:1], ALU.subtract,
               [p_name, ("shc", ci)], [d_name])
            if has_s:
                p3 = p_ap[R, npr:N].rearrange("p (s t) -> p s t", t=SL)
                d3 = d_ap[R, npr:N].rearrange("p (s t) -> p s t", t=SL)
                tt("pool", d3[:, :, 1:SL], p3[:, :, 0:SL - 1], p3[:, :, 1:SL], ALU.subtract, [p_name], [d_name])
                tt("pool", d3[:, :, 0:1], shc[R, ci, 0:16].rearrange("p (s o) -> p s o", o=1), p3[:, :, 0:1],
                   ALU.subtract, [p_name, ("shc", ci)], [d_name])
            stt(out_ap[R, 0:N], d_ap[R, 0:N], cols[R, C_MU + ci:C_MU + ci + 1], p_ap[R, 0:N],
                ALU.mult, ALU.add, [d_name, p_name, "cols"], [out_name])
            cp("pool", shc[R, ci, 16:17], p_ap[R, npr - 1:npr], [p_name], [("shc", ci)])
            if has_s:
                cp("pool", shc[R, ci, 0:16].rearrange("p (s o) -> p s o", o=1), p3[:, :, SL - 1:SL],
                   [p_name], [("shc", ci)])

        def conv3(ti, u_ap, u_name, acc_ap, acc_name, cc, cc_name, wc0, skip_first=False):
            pos0, chunks, has_s = TILES[ti]
            npr = sum(chunks)
            N = npr + (64 if has_s else 0)
            w0, w1, w2 = wc0
            if not skip_first:
                ts("dve", acc_ap[:, 0:N], u_ap[:, 0:N], w2, None, ALU.mult, None, [u_name, "cols"], [acc_name])
            stt(acc_ap[:, 1:npr], u_ap[:, 0:npr - 1], w1, acc_ap[:, 1:npr], ALU.mult, ALU.add,
                [u_name, "cols", acc_name], [acc_name])
            stt(acc_ap[:, 2:npr], u_ap[:, 0:npr - 2], w0, acc_ap[:, 2:npr], ALU.mult, ALU.add,
                [u_name, "cols", acc_name], [acc_name])
            stt(acc_ap[:, 0:1], cc[:, 16, 1:2], w1, acc_ap[:, 0:1], ALU.mult, ALU.add,
                [cc_name, "cols", acc_name], [acc_name])
            stt(acc_ap[:, 0:2], cc[:, 16, 0:2], w0, acc_ap[:, 0:2], ALU.mult, ALU.add,
                [cc_name, "cols", acc_name], [acc_name])
            if has_s:
                u3 = u_ap[:, npr:N].rearrange("p (s t) -> p s t", t=SL)
                a3 = acc_ap[:, npr:N].rearrange("p (s t) -> p s t", t=SL)
                stt(a3[:, :, 1:SL], u3[:, :, 0:SL - 1], w1, a3[:, :, 1:SL], ALU.mult, ALU.add,
                    [u_name, "cols", acc_name], [acc_name])
                stt(a3[:, :, 2:SL], u3[:, :, 0:SL - 2], w0, a3[:, :, 2:SL], ALU.mult, ALU.add,
                    [u_name, "cols", acc_name], [acc_name])
                stt(a3[:, :, 0:1], cc[:, 0:16, 1:2], w1, a3[:, :, 0:1], ALU.mult, ALU.add,
                    [cc_name, "cols", acc_name], [acc_name])
                stt(a3[:, :, 0:2], cc[:, 0:16, 0:2], w0, a3[:, :, 0:2], ALU.mult, ALU.add,
                    [cc_name, "cols", acc_name], [acc_name])
                cp("pool", cc[:, 0:16, :], u3[:, :, SL - 2:SL], [u_name], [cc_name])
            cp("pool", cc[:, 16, :], u_ap[:, npr - 2:npr], [u_name], [cc_name])

        def p1_gen(ch, par, slot, C, col0, res, mask_ap=None, mask_name="maska", nl=None):
            if mask_ap is None:
                mask_ap = maska
            at = blk[:, par, 0, slot, :]
            bt = blk[:, par, 1, slot, :]
            kt = blk[:, par, 2, slot, :]
            vb = blk[:, par, 3, slot, :]
            BL = ("blk", par)
            RT = ("rtb", par)
            rts = rtb[:, par, col0:col0 + C]
            am = amat[:, ch, :]
            AM = ("amat", ch)
            grp = "scan%d" % ch
            bA = bank(grp)
            pA = ps[bA]
            nA = ("ps", bA)
            mm(pA[:, 0:128], bt, at, True, True, [BL], [nA])
            mm(pA[:, 384:384 + C], bt, rts, True, True, [BL, RT], [nA])
            mm(pA[:, 256:384], kt, at, True, True, [BL], [nA])
            mm(pA[:, 448:448 + C], kt, rts, True, True, [BL, RT], [nA])
            mm(pA[:, 128:256], at, bt, True, True, [BL], [nA])
            if C == 64:
                tt("dve", am, pA[:, :], mask_ap[:, :], ALU.mult, [nA, mask_name], [AM])
            else:
                tt("dve", am[:, 0:384], pA[:, 0:384], mask_ap[:, 0:384], ALU.mult, [nA, mask_name], [AM])
                tt("dve", am[:, 384:384 + C], pA[:, 384:384 + C], mask_ap[:, 384:384 + C], ALU.mult,
                   [nA, mask_name], [AM])
                tt("dve", am[:, 448:448 + C], pA[:, 448:448 + C], mask_ap[:, 448:448 + C], ALU.mult,
                   [nA, mask_name], [AM])
            mi = 0
            tt("dve", msb[:, ch, mi, :], am[:, 0:128], identb[:, :], ALU.add, [AM, "identb"], [("msb", ch, mi)])
            yield
            bT = bank(grp)
            nT = ("ps", bT)
            pT = ps[bT][:, 0:192].bitcast(BF16)
            tr(pT[:, 0:128], vb, [BL], [nT])
            tr(pT[:, 128:256], bt, [BL], [nT])
            tr(pT[:, 256:384], kt, [BL], [nT])
            TM = ("tmsb", ch)
            cp("act", tmsb[:, ch, :], pT[:, 0:384], [nT], [TM])
            res["tm"] = (tmsb[:, ch, :], TM)
            yield
            if nl is None:
                nl = 1
                while (1 << nl) < C:
                    nl += 1
            Pp, Qp, pname = am[:, 0:128], am[:, 128:256], AM
            for k in range(1, nl):
                last = (k == nl - 1)
                bB = bank(grp)
                nB = ("ps", bB)
                if not last:
                    mm(ps[bB][:, 0:128], Qp, Pp, True, True, [pname], [nB])
                mm(ps[bB][:, 128:256], Pp, Qp, True, True, [pname], [nB])
                dst = pq[:, ch, k % 2, :]
                dname = ("pq", ch, k % 2)
                if not last:
                    cp("act", dst, ps[bB][:, 0:256], [nB], [dname])
                else:
                    cp("act", dst[:, 128:256], ps[bB][:, 128:256], [nB], [dname])
                Pp, Qp, pname = dst[:, 0:128], dst[:, 128:256], dname
                yield
                mm(ps[bB][:, 256:384], Qp, msb[:, ch, mi, :], True, True, [dname, ("msb", ch, mi)], [nB])
                tt("dve", msb[:, ch, 1 - mi, :], msb[:, ch, mi, :], ps[bB][:, 256:384], ALU.add,
                   [nB, ("msb", ch, mi)], [("msb", ch, 1 - mi)])
                mi = 1 - mi
                yield
            res["M"] = (msb[:, ch, mi, :], ("msb", ch, mi))
            res["am"] = (am, AM)
            yield

        def p2_gen(ch, par, slot, C, col0, res, H_ap, H_name, Hb_ap, Hb_name, y_ap, y_name, W_ap, W_name,
                   lc=0, rowmask=None):
            at = blk[:, par, 0, slot, :]
            BL = ("blk", par)
            RT = ("rtb", par)
            rts = rtb[:, par, col0:col0 + C]
            am, AM = res["am"]
            tmv, TM = res["tm"]
            M_ap, M_name = res["M"]
            vtm, btm, ktm = tmv[:, 0:128], tmv[:, 128:256], tmv[:, 256:384]
            grp = ("scan0", "scan1", "tm", "big")[ch]
            bX = bank(grp)
            nX = ("ps", bX)
            pX = ps[bX]
            XS, US, HT = ("xsb", ch), ("usb", ch), ("htmp", ch)
            mm(pX[:, 0:128], at, Hb_ap, True, False, [BL, Hb_name], [nX])
            mm(pX[:, 0:128], am[:, 256:384], vtm, False, True, [AM, TM], [nX])
            cp("act", xsb[:, ch, :], pX[:, 0:128], [nX], [XS])
            if rowmask is not None:
                ts("dve", vgs[:, ch, :], vtm, rowmask, None, ALU.mult, None, [TM, "segmask"], [("vgs", ch)])
                vH, vHn = vgs[:, ch, :], ("vgs", ch)
            else:
                vH, vHn = vtm, TM
            yield
            mm(pX[:, 128:256], M_ap, xsb[:, ch, :], True, True, [M_name, XS], [nX])
            if rowmask is not None:
                ts("dve", usb[:, ch, :], pX[:, 128:256], rowmask, None, ALU.mult, None, [nX, "segmask"], [US])
            else:
                cp("dve", usb[:, ch, :], pX[:, 128:256], [nX], [US])
            yield
            mm(pX[:, 384:512], btm, usb[:, ch, :], True, False, [TM, US], [nX])
            mm(pX[:, 384:512], ktm, vH, False, True, [TM, vHn], [nX])
            bY = bank(grp)
            nY = ("ps", bY)
            pY = ps[bY]
            mm(pY[:, 0:C], Hb_ap, rts, True, False, [Hb_name, RT], [nY])
            mm(pY[:, 0:C], usb[:, ch, :], am[:, 384 + lc:384 + lc + C], False, False, [US, AM], [nY])
            mm(pY[:, 0:C], vtm, am[:, 448 + lc:448 + lc + C], False, True, [TM, AM], [nY])
            Wc = W_ap[:, col0 + C - 1:col0 + C]
            tt("dve", htmp[:, ch, :], pX[:, 384:512], H_ap, ALU.add, [nX, H_name], [HT])
            act(Hb_ap, htmp[:, ch, :], AF.Identity, [HT, W_name], [Hb_name], scale=Wc)
            act(H_ap, htmp[:, ch, :], AF.Identity, [HT, W_name], [H_name], scale=Wc)
            cp("act", y_ap[:, col0:col0 + C], pY[:, 0:C], [nY], [y_name])
            yield

        def scan_gen(ch, par, slot, C, col0, H_ap, H_name, Hb_ap, Hb_name, y_ap, y_name, W_ap, W_name):
            res = {}
            yield from p1_gen(ch, par, slot, C, col0, res)
            yield from p2_gen(ch, par, slot, C, col0, res, H_ap, H_name, Hb_ap, Hb_name, y_ap, y_name, W_ap, W_name)

        def run_gens(gens):
            gens = list(gens)
            while gens:
                for g in list(gens):
                    try:
                        next(g)
                    except StopIteration:
                        gens.remove(g)

        for ti, (pos0, chunks, has_s) in enumerate(TILES):
            npr = sum(chunks)
            N = npr + (64 if has_s else 0)
            nblk = (N + 127) // 128
            rmi = 0 if ti == 0 else (2 if has_s else 1)
            store_pieces = []
            for b in range(nblk):
                store_pieces.append(load_x_block(ti, b))
            run_gens([rms_gen(x_tm[:, b, :], [("x", b)], C_G1, b, xn_fm, ["xn_fm"]) for b in range(nblk)])
            XN = ["xn_fm"]
            for half in range(2):
                c0 = 3072 + half * 256
                nc_ = 256 if half == 0 else 128
                slot = wjob(w_in, 0, 8, c0, nc_)
                for q in range(nc_ // 128):
                    c = half * 2 + q
                    bk = proj_fm(slot, q * 128, xn_fm, XN, N)
                    cp("act", lor_p[:, c, 0:N], ps[bk][:, 0:N], [("ps", bk)], [("lor_p", c)])
                    token_shift(ti, lor_p[:, c, :], ("lor_p", c), T(0), "t0", lor_x[:, c, :], ("lor_x", c), 24 + c,
                                nrows=128 if c < 2 else 32)
            act(lor_x[0:64, 0, 0:N], lor_x[0:64, 0, 0:N], AF.Tanh, [("lor_x", 0)], [("lor_x", 0)])
            act(lor_x[:, 1, 0:N], lor_x[:, 1, 0:N], AF.Sigmoid, [("lor_x", 1)], [("lor_x", 1)])
            act(lor_x[0:32, 2, 0:N], lor_x[0:32, 2, 0:N], AF.Sigmoid, [("lor_x", 2)], [("lor_x", 2)])
            z_fm = reg1[:, 0:8, :]
            WT = [13, 21]
            YT = [3, 22]
            BN = [17, 23]
            GT = [9, 24]

            def tn(i):
                return "t%d" % i

            IDS = [(1, 2, 3, 0, 4, 5, 6), (25, 26, 27, 28, 18, 19, 20)]

            def prep_ls(m, par):
                Wt, bon, gt = WT[par], BN[par], GT[par]
                lwi = m % 2
                dma("sp", lwp[:, lwi, :, :], lora_pk[m], (), [("lw", lwi)])
                LW = ("lw", lwi)
                bk = bank("big")
                mm(ps[bk][:, 0:N], lwp[0:64, lwi, 0, :], lor_x[0:64, 0, 0:N], True, True,
                   [LW, ("lor_x", 0)], [("ps", bk)])
                act(T(7)[:, 0:N], ps[bk][:, 0:N], AF.Sigmoid, [("ps", bk), "cols"], ["t7"], bias=col(C_W0 + m))
                bk = bank("big")
                mm(ps[bk][:, 0:N], lwp[64:128, lwi, 0, :], lor_x[64:128, 0, 0:N], True, True,
                   [LW, ("lor_x", 0)], [("ps", bk)])
                act(T(8)[:, 0:N], ps[bk][:, 0:N], AF.Sigmoid, [("ps", bk), "cols"], ["t8"], bias=col(C_A0 + m))
                bk = bank("big")
                mm(ps[bk][:, 0:N], lwp[:, lwi, 1, :], lor_x[:, 1, 0:N], True, False,
                   [LW, ("lor_x", 1)], [("ps", bk)])
                mm(ps[bk][:, 0:N], lwp[0:32, lwi, 2, :], lor_x[0:32, 2, 0:N], False, True,
                   [LW, ("lor_x", 2)], [("ps", bk)])
                cp("act", T(gt)[:, 0:N], ps[bk][:, 0:N], [("ps", bk)], [tn(gt)])
                S.op("dve", lambda e, o=T(12)[:, 0:N], d0=rmask[:, rmi, 0:N], d1=T(7)[:, 0:N]:
                     e.tensor_tensor_scan(o, d0, d1, 0.0, ALU.mult, ALU.add), ["rmask", "t7"], ["t12"])
                act(T(Wt)[:, 0:N], T(12)[:, 0:N], AF.Exp, ["t12"], [tn(Wt)], scale=-EXPM05)
                act(T(14)[:, 0:N], T(12)[:, 0:N], AF.Exp, ["t12"], ["t14"], scale=EXPM05)
                tt("pool", T(15)[:, 0:N], T(12)[:, 0:N], T(7)[:, 0:N], ALU.subtract, ["t12", "t7"], ["t15"])
                act(T(15)[:, 0:N], T(15)[:, 0:N], AF.Exp, ["t15"], ["t15"], scale=-EXPM05)

            def prep_front(m, pm, slots):
                ids = IDS[pm]
                for which in range(3):
                    bk = proj_fm(slots[which], pm * 128, xn_fm, XN, N)
                    raw = T(ids[which])
                    rn_ = tn(ids[which])
                    cp("act", raw[:, 0:N], ps[bk][:, 0:N], [("ps", bk)], [rn_])
                    token_shift(ti, raw, rn_, T(ids[3]), tn(ids[3]), T(ids[4 + which]), tn(ids[4 + which]),
                                which * 8 + m)

            def prep_back(m, par, pm):
                Wt, bon, gt = WT[par], BN[par], GT[par]
                rid, kid, vid = IDS[pm][4:7]
                ts("dve", T(10)[:, 0:N], T(kid)[:, 0:N], col(C_KK + m), None, ALU.mult, None, [tn(kid), "cols"], ["t10"])
                tt("pool", T(11)[:, 0:N], T(10)[:, 0:N], T(10)[:, 0:N], ALU.mult, ["t10"], ["t11"])
                bk = bank("big")
                mm(ps[bk][:, 0:N], bones[:, :], T(11)[:, 0:N], True, True, ["bones", "t11"], [("ps", bk)])
                act(T(11)[:, 0:N], ps[bk][:, 0:N], AF.Ln, [("ps", bk), "epsc"], ["t11"], bias=epsc[:, 2:3])
                act(T(11)[:, 0:N], T(11)[:, 0:N], AF.Exp, ["t11"], ["t11"], scale=-0.5)
                stt(T(10)[:, 0:N], T(10)[:, 0:N], -1.0, T(11)[:, 0:N], ALU.mult, ALU.mult, ["t10", "t11"], ["t10"])
                ts("dve", T(11)[:, 0:N], T(8)[:, 0:N], col(C_KA + m), col(C_OMKA + m), ALU.mult, ALU.add,
                   ["t8", "cols"], ["t11"])
                tt("dve", T(kid)[:, 0:N], T(kid)[:, 0:N], T(11)[:, 0:N], ALU.mult, [tn(kid), "t11"], [tn(kid)])
                stt(T(11)[:, 0:N], T(10)[:, 0:N], -1.0, T(8)[:, 0:N], ALU.mult, ALU.mult, ["t10", "t8"], ["t11"])
                tt("dve", rtb[:, par, 0:N], T(rid)[:, 0:N], T(Wt)[:, 0:N], ALU.mult, [tn(rid), tn(Wt)], [("rtb", par)])
                stt(T(bon)[:, 0:N], T(rid)[:, 0:N], col(C_RK + m), T(kid)[:, 0:N], ALU.mult, ALU.mult,
                    [tn(rid), "cols", tn(kid)], [tn(bon)])
                bk = bank("big")
                mm(ps[bk][:, 0:N], bones[:, :], T(bon)[:, 0:N], True, True, ["bones", tn(bon)], [("ps", bk)])
                tt("dve", T(bon)[:, 0:N], ps[bk][:, 0:N], T(vid)[:, 0:N], ALU.mult, [("ps", bk), tn(vid)], [tn(bon)])

            def fill(par, regions, s0, zero_first, kid=5, vid=6):
                BL = ("blk", par)
                if zero_first:
                    memset("pool", blk[:, par, :, :, :], 0.0, [BL])
                for (c0_, nch, C) in regions:
                    for hh in range(2):
                        P_ = slice(hh * 64, hh * 64 + 64)

                        def v3(tix):
                            return T(tix)[P_, c0_:c0_ + nch * C].rearrange("p (n c) -> p n c", c=C)

                        def o3(bi):
                            return blk[P_, par, bi, s0:s0 + nch, hh * 64:hh * 64 + C]
                        tt("dve", o3(0), v3(10), v3(15), ALU.mult, ["t10", "t15"], [BL])
                        tt("dve", o3(1), v3(11), v3(14), ALU.mult, ["t11", "t14"], [BL])
                        tt("pool", o3(2), v3(kid), v3(14), ALU.mult, [tn(kid), "t14"], [BL])
                        cp("pool", o3(3), v3(vid), [tn(vid)], [BL])
                    s0 += nch

            def prompt_gen(m, par):
                c_ = 0
                for si, C in enumerate(chunks):
                    yield from scan_gen(par, par, si, C, c_, hblk[:, m, :], ("H", m), hbb[:, m, :], ("Hb", m),
                                        T(YT[par]), tn(YT[par]), T(WT[par]), tn(WT[par]))
                    c_ += C

            def sample_part(m, par):
                HS = [("hsb", i) for i in range(8)]
                HSB = [("hsbb", i) for i in range(8)]
                slot_s = len(chunks)
                res = {}
                for _ in p1_gen(0, par, slot_s, 64, npr, res, mask_ap=maskas, mask_name="maskas", nl=2):
                    pass

                def p2_chain(ch, qs, sb0):
                    for q in qs:
                        sg = sb0 + q
                        yield from p2_gen(ch, par, slot_s, SL, npr + sg * SL, res, hsb[:, q, :], ("hsb", q),
                                          hsbb[:, q, :], ("hsbb", q), T(YT[par]), tn(YT[par]),
                                          T(WT[par]), tn(WT[par]), lc=sg * SL, rowmask=segmask[:, sg:sg + 1])
                for hv in range(2):
                    sb0 = hv * 8
                    for hh in range(2):
                        dma("sp", sld[hh * 64:hh * 64 + 64, :, hh * 64:hh * 64 + 64],
                            st_wkv[sb0:sb0 + 8, 2 * m + hh, :, :].rearrange("s i j -> i s j"), (), ["sld"])
                    for g4 in range(2):
                        bk = bank("tm")
                        for q in range(4):
                            tr(ps[bk][:, q * 128:(q + 1) * 128], sld[:, g4 * 4 + q, :], ["sld"], [("ps", bk)])
                        cp("act", hsb[:, g4 * 4:g4 * 4 + 4, :],
                           ps[bk][:, :].rearrange("p (k t) -> p k t", t=128), [("ps", bk)], HS[g4 * 4:g4 * 4 + 4])
                        cp("dve", hsbb[:, g4 * 4:g4 * 4 + 4, :],
                           ps[bk][:, :].rearrange("p (k t) -> p k t", t=128), [("ps", bk)], HSB[g4 * 4:g4 * 4 + 4])
                    run_gens([p2_chain(0, [0, 4], sb0), p2_chain(1, [1, 5], sb0),
                              p2_chain(2, [2, 6], sb0), p2_chain(3, [3, 7], sb0)])
                    for g4 in range(2):
                        bk = bank("tm")
                        for q in range(4):
                            tr(ps[bk][:, q * 128:(q + 1) * 128], hsb[:, g4 * 4 + q, :], [HS[g4 * 4 + q]], [("ps", bk)])
                        cp("act", sld[:, g4 * 4:g4 * 4 + 4, :],
                           ps[bk][:, :].rearrange("p (k t) -> p k t", t=128), [("ps", bk)], ["sld"])
                    for hh in range(2):
                        dma("pool", wkv_s[sb0:sb0 + 8, 2 * m + hh, :, :].rearrange("s i j -> i s j"),
                            sld[hh * 64:hh * 64 + 64, :, hh * 64:hh * 64 + 64], ["sld"], ())

            def post(m, par):
                yt, bon, gt = YT[par], BN[par], GT[par]
                q1, q2 = (1, 2) if par == 0 else (18, 19)
                y = T(yt)
                bk = bank("big")
                mm(ps[bk][:, 0:N], bones[:, :], y[:, 0:N], True, True, ["bones", tn(yt)], [("ps", bk)])
                stt(T(q1)[:, 0:N], ps[bk][:, 0:N], -1.0 / 64.0, y[:, 0:N], ALU.mult, ALU.add,
                    [("ps", bk), tn(yt)], [tn(q1)])
                tt("pool", T(q2)[:, 0:N], T(q1)[:, 0:N], T(q1)[:, 0:N], ALU.mult, [tn(q1)], [tn(q2)])
                yield
                bk = bank("big")
                mm(ps[bk][:, 0:N], bones[:, :], T(q2)[:, 0:N], True, True, ["bones", tn(q2)], [("ps", bk)])
                act(T(q2)[:, 0:N], ps[bk][:, 0:N], AF.Ln, [("ps", bk), "epsc"], [tn(q2)],
                    bias=col_eps_gn, scale=1.0 / 64.0)
                act(T(q2)[:, 0:N], T(q2)[:, 0:N], AF.Exp, [tn(q2)], [tn(q2)], scale=-0.5)
                yield
                tt("dve", T(q1)[:, 0:N], T(q1)[:, 0:N], T(q2)[:, 0:N], ALU.mult, [tn(q1), tn(q2)], [tn(q1)])
                ts("dve", T(q1)[:, 0:N], T(q1)[:, 0:N], col(C_LG + m), col(C_LB + m), ALU.mult, ALU.add,
                   [tn(q1), "cols"], [tn(q1)])
                tt("dve", T(q1)[:, 0:N], T(q1)[:, 0:N], T(bon)[:, 0:N], ALU.add, [tn(q1), tn(bon)], [tn(q1)])
                tt("dve", z_fm[:, m, 0:N], T(q1)[:, 0:N], T(gt)[:, 0:N], ALU.mult,
                   [tn(q1), tn(gt)], [("z", m)])

            zb_fm = reg1[:, 8:16, :]

            def sc_group_gen(pg):
                slots = [wjob(w_in, 0, 8, RW + which * 1024 + pg * 256, 256) for which in range(3)]
                yield
                for pm in range(2):
                    c = pg * 2 + pm
                    bh = proj_fm(slots[0], pm * 128, xn_fm, XN, N)
                    cp("act", T(25)[:, 0:N], ps[bh][:, 0:N], [("ps", bh)], ["t25"])
                    yield
                    bb = proj_fm(slots[1], pm * 128, xn_fm, XN, N)
                    cp("act", T(26)[:, 0:N], ps[bb][:, 0:N], [("ps", bb)], ["t26"])
                    yield
                    bc = proj_fm(slots[2], pm * 128, xn_fm, XN, N)
                    tt("dve", T(27)[:, 0:N], ps[bc][:, 0:N], T(25)[:, 0:N], ALU.mult, [("ps", bc), "t25"], ["t27"])
                    yield
                    conv3(ti, T(27), "t27", T(28), "t28", scc[:, c, :, :], ("scc", c),
                          (col(C_CS + 0 * 8 + c), col(C_CS + 1 * 8 + c), col(C_CS + 2 * 8 + c)))
                    tt("dve", zb_fm[:, c, 0:N], T(28)[:, 0:N], T(26)[:, 0:N], ALU.mult, ["t28", "t26"],
                       [("zb", c)])
                    yield

            regions = []
            c_ = 0
            for C in (chunks + [64] if has_s else chunks):
                if regions and regions[-1][2] == C:
                    regions[-1] = (regions[-1][0], regions[-1][1] + 1, C)
                else:
                    regions.append((c_, 1, C))
                c_ += C
            for pg in range(4):
                slots = [wjob(w_in, 0, 8, which * 1024 + pg * 256, 256) for which in range(3)]
                prep_ls(pg * 2, 0)
                prep_front(pg * 2, 0, slots)
                prep_front(pg * 2 + 1, 1, slots)
                prep_back(pg * 2, 0, 0)
                fill(0, regions, 0, False, IDS[0][5], IDS[0][6])
                prep_ls(pg * 2 + 1, 1)
                prep_back(pg * 2 + 1, 1, 1)
                fill(1, regions, 0, False, IDS[1][5], IDS[1][6])
                run_gens([prompt_gen(pg * 2 + pm, pm) for pm in range(2)] + [sc_group_gen(pg)])
                if has_s:
                    for pm in range(2):
                        sample_part(pg * 2 + pm, pm)
                run_gens([post(pg * 2 + pm, pm) for pm in range(2)])
            mg_fm = reg1[:, 16:24, :]
            ZN = [("z", m) for m in range(8)]
            ZBN = [("zb", m) for m in range(8)]
            def e_chunk(c, pm, s_a, s_b, s_ga, s_gb, t4):
                a1, a2, a3, a4 = t4
                bk = proj_fm(s_ga, pm * 128, xn_fm, XN, N, grp="wide")
                act(T(a1)[:, 0:N], ps[bk][:, 0:N], AF.Sigmoid, [("ps", bk), "cols"], [tn(a1)], bias=col(C_BG + c))
                yield
                bk = proj_fm(s_gb, pm * 128, xn_fm, XN, N, grp="wide")
                act(T(a2)[:, 0:N], ps[bk][:, 0:N], AF.Sigmoid, [("ps", bk), "cols"], [tn(a2)],
                    bias=col(C_BG + 8 + c))
                yield
                bk = proj_fm(s_a, pm * 128, z_fm, ZN, N, grp="wide")
                tt("dve", T(a3)[:, 0:N], ps[bk][:, 0:N], T(a1)[:, 0:N], ALU.mult, [("ps", bk), tn(a1)], [tn(a3)])
                yield
                bk = proj_fm(s_b, pm * 128, zb_fm, ZBN, N, grp="wide")
                tt("dve", T(a4)[:, 0:N], ps[bk][:, 0:N], T(a2)[:, 0:N], ALU.mult, [("ps", bk), tn(a2)], [tn(a4)])
                tt("dve", mg_fm[:, c, 0:N], T(a3)[:, 0:N], T(a4)[:, 0:N], ALU.add, [tn(a3), tn(a4)], [("mg", c)])
                yield

            TSETS = [(1, 2, 3, 4), (5, 6, 7, 8), (9, 10, 11, 12), (13, 14, 15, 16)]
            for pg in range(4):
                s_a = wjob(w_br, 0, 8, pg * 256, 256)
                s_b = wjob(w_bs, 0, 8, pg * 256, 256)
                s_ga = wjob(w_in, 0, 8, RW + 3072 + pg * 256, 256)
                s_gb = wjob(w_in, 0, 8, RW + 3072 + 1024 + pg * 256, 256)
                run_gens([e_chunk(pg * 2 + pm, pm, s_a, s_b, s_ga, s_gb, TSETS[pm]) for pm in range(2)])
            MGN = [("mg", m) for m in range(8)]
            for cg in range(4):
                slot = wjob(w_out, 0, 8, cg * 256, 256)
                for b in range(nblk):
                    bk = bank("tm")
                    for kc in range(8):
                        mm(ps[bk][:, 0:256], mg_fm[:, kc, b * 128:(b + 1) * 128],
                           wring[:, slot, kc, 0:256], kc == 0, kc == 7,
                           [("w", slot)] + MGN, [("ps", bk)])
                    tt("dve", x_tm[:, b, cg * 256:(cg + 1) * 256], x_tm[:, b, cg * 256:(cg + 1) * 256],
                       ps[bk][:, 0:256], ALU.add, [("ps", bk), ("x", b)], [("x", b)])
            run_gens([rms_gen(x_tm[:, b, :], [("x", b)], C_G2, b, xn_fm, ["xn_fm"]) for b in range(nblk)])
            h_fm = reg1[:, 0:22, :]
            def g_chunk(c, pm, s_g, s_v, t4):
                a1, a2, a3, a4 = t4
                bk = proj_fm(s_g, pm * 128, xn_fm, XN, N, grp="wide")
                cp("act", T(a1)[:, 0:N], ps[bk][:, 0:N], [("ps", bk)], [tn(a1)])
                act(T(a2)[:, 0:N], ps[bk][:, 0:N], AF.Identity, [("ps", bk), "cols"], [tn(a2)],
                    scale=col(C_CF + 2 * 44 + c))
                yield
                conv3(ti, T(a1), tn(a1), T(a2), tn(a2), ffc[:, c, :, :], ("ffc", c),
                      (col(C_CF + 0 * 44 + c), col(C_CF + 1 * 44 + c), col(C_CF + 2 * 44 + c)), skip_first=True)
                act(T(a2)[:, 0:N], T(a2)[:, 0:N], AF.Silu, [tn(a2)], [tn(a2)])
                yield
                bk = proj_fm(s_v, pm * 128, xn_fm, XN, N, grp="wide")
                cp("act", T(a3)[:, 0:N], ps[bk][:, 0:N], [("ps", bk)], [tn(a3)])
                cv = 22 + c
                act(T(a4)[:, 0:N], ps[bk][:, 0:N], AF.Identity, [("ps", bk), "cols"], [tn(a4)],
                    scale=col(C_CF + 2 * 44 + cv))
                yield
                conv3(ti, T(a3), tn(a3), T(a4), tn(a4), ffc[:, cv, :, :], ("ffc", cv),
                      (col(C_CF + 0 * 44 + cv), col(C_CF + 1 * 44 + cv), col(C_CF + 2 * 44 + cv)), skip_first=True)
                tt("dve", h_fm[:, c, 0:N], T(a2)[:, 0:N], T(a4)[:, 0:N], ALU.mult, [tn(a2), tn(a4)], [("h", c)])
                yield

            for g0 in range(0, 11, 2):
                gens = []
                for gi, g in enumerate(range(g0, min(g0 + 2, 11))):
                    s_g = wjob(w_up, 0, 8, g * 256, 256)
                    s_v = wjob(w_up, 0, 8, DFF + g * 256, 256)
                    for pm in range(2):
                        gens.append(g_chunk(g * 2 + pm, pm, s_g, s_v, TSETS[gi * 2 + pm]))
                run_gens(gens)
            HN = [("h", c) for c in range(22)]
            for cg in range(4):
                kparts = [(0, 8), (8, 8), (16, 6)]
                slots = [wjob(w_down, k0 * 128, nk, cg * 256, 256) for (k0, nk) in kparts]
                for b in range(nblk):
                    bk = bank("tm")
                    for pi, (k0, nk) in enumerate(kparts):
                        for kk_ in range(nk):
                            kc = k0 + kk_
                            mm(ps[bk][:, 0:256], h_fm[:, kc, b * 128:(b + 1) * 128],
                               wring[:, slots[pi], kk_, 0:256], kc == 0, kc == 21,
                               [("w", slots[pi])] + HN, [("ps", bk)])
                    tt("dve", x_tm[:, b, cg * 256:(cg + 1) * 256], x_tm[:, b, cg * 256:(cg + 1) * 256],
                       ps[bk][:, 0:256], ALU.add, [("ps", bk), ("x", b)], [("x", b)])
            run_gens([fin_gen(b, store_pieces[b]) for b in range(nblk)])

        for g4 in range(2):
            bk = bank("tm")
            for q in range(4):
                tr(ps[bk][:, q * 128:(q + 1) * 128], hblk[:, g4 * 4 + q, :], [("H", g4 * 4 + q)], [("ps", bk)])
            cp("act", sld[:, g4 * 4:g4 * 4 + 4, :], ps[bk][:, :].rearrange("p (k t) -> p k t", t=128),
               [("ps", bk)], ["sld"])
        wv = wkv_p.rearrange("(m two) i j -> two i m j", two=2)
        for hh in range(2):
            dma("pool", wv[hh], sld[hh * 64:hh * 64 + 64, 0:8, hh * 64:hh * 64 + 64], ["sld"], ())
        for c in range(27):
            w = 128 if c < 26 else 32
            b = bank("tm")
            tr(ps[b][0:17, 0:w], shc[0:w, c, :], [("shc", c)], [("ps", b)])
            cp("act", tm_scr[0:17, (c % 8) * 128:(c % 8) * 128 + w], ps[b][0:17, 0:w], [("ps", b)], ["tm_scr"])
            if c % 8 == 7 or c == 26:
                c0_ = (c // 8) * 8 * 128
                wp = min(1024, RW - c0_)
                dma("pool", shift_o[:, c0_:c0_ + wp], tm_scr[0:17, 0:wp], ["tm_scr"], ())
        for c in range(8):
            b = bank("tm")
            tr(ps[b][0:34, 0:128], scc[:, c, :, :].rearrange("p s r -> p (s r)"), [("scc", c)], [("ps", b)])
            cp("act", hsb[0:34, c, :], ps[b][0:34, 0:128], [("ps", b)], ["hsb"])
        dma("pool", sc_o.rearrange("r (c f) -> r c f", f=128), hsb[0:34, 0:8, :], ["hsb"], ())
        for c in range(44):
            b = bank("tm")
            tr(ps[b][0:34, 0:128], ffc[:, c, :, :].rearrange("p s r -> p (s r)"), [("ffc", c)], [("ps", b)])
            cp("act", tm_scr[0:34, (c % 8) * 128:(c % 8 + 1) * 128], ps[b][0:34, 0:128], [("ps", b)], ["tm_scr"])
            if c % 8 == 7 or c == 43:
                c0_ = (c // 8) * 8 * 128
                wp = min(1024, 2 * DFF - c0_)
                dma("pool", ffn_o[:, c0_:c0_ + wp], tm_scr[0:34, 0:wp], ["tm_scr"], ())

        final_waits = [(("d", k), S.dval[k]) for k in range(Sched.NDS) if S.dval[k] > 0]

        cum = {}
        for en in Sched.ENG:
            c = 0
            for fn, waits, me in S.streams[en]:
                if (not me.dma) and (me.semkey, me.val) in S.flag:
                    c += 1
                    cum[(me.semkey, me.val)] = c

        def emit(name, e):
            for fn, waits, me in S.streams[name]:
                for semkey, val, isdma in waits:
                    e.wait_ge(sems[semkey], val if isdma else cum[(semkey, val)])
                ins = fn(e)
                if me.dma:
                    ins.then_inc(sems[me.semkey], 16)
                elif (me.semkey, me.val) in S.flag:
                    ins.then_inc(sems[me.semkey], 1)

        with nc.Block() as block:
            @block.tensor
            def _(e):
                emit("pe", e)

            @block.scalar
            def _(e):
                emit("act", e)

            @block.vector
            def _(e):
                emit("dve", e)

            @block.gpsimd
            def _(e):
                emit("pool", e)

            @block.sync
            def _(e):
                emit("sp", e)
                for semkey, val in final_waits:
                    e.wait_ge(sems[semkey], val)
    return nc, recorded


def _consts():
    ident = np.eye(128, dtype=np.float32)
    bones = np.zeros((128, 128), np.float32)
    bones[:64, :64] = 1.0
    bones[64:, 64:] = 1.0
    s = np.arange(64)
    su = (s[:, None] < s[None, :]).astype(np.float32)
    sl = (s[:, None] > s[None, :]).astype(np.float32)
    incl = (s[:, None] <= s[None, :]).astype(np.float32)
    def bd(a):
        o = np.zeros((128, 128), np.float32)
        o[:64, :64] = a
        o[64:, 64:] = a
        return o
    maska = np.concatenate([bd(su), bd(sl), bd(su), np.concatenate([incl, incl], 0),
                            np.concatenate([incl, incl], 0)], axis=1)
    rmask = np.ones((128, 3, NW), np.float32)
    def starts(chunks, has_s):
        st, c = [], 0
        for C in chunks:
            st.append(c); c += C
        if has_s:
            for sg in range(NSEG):
                st.append(c + sg * SL)
        return st
    rmask[:, 0, starts(TILES[0][1], False)] = 0.0
    rmask[:, 1, starts(TILES[1][1], False)] = 0.0
    rmask[:, 2, starts(TILES[5][1], True)] = 0.0
    sg = s // SL
    same = (sg[:, None] == sg[None, :])
    su_s, sl_s, incl_s = su * same, sl * same, incl * same
    maskas = np.concatenate([bd(su_s), bd(sl_s), bd(su_s), np.concatenate([incl_s, incl_s], 0),
                             np.concatenate([incl_s, incl_s], 0)], axis=1).astype(np.float32)
    segmask = np.zeros((128, NSEG), np.float32)
    for g in range(NSEG):
        for hh in range(2):
            segmask[hh * 64 + g * SL:hh * 64 + (g + 1) * SL, g] = 1.0
    return ident, bones, np.ascontiguousarray(maska), rmask, np.ascontiguousarray(maskas), segmask


_NC_CACHE = {}


def kernel(**inp):
    f = lambda k: np.ascontiguousarray(np.asarray(inp[k], dtype=np.float32))
    x_prompt, x_sample = f("x_prompt"), f("x_sample")
    ident, bones, maska, rmask, maskas, segmask = _consts()

    def colfm(v, nchunk):
        v = np.asarray(v, np.float32).reshape(-1)
        pad = nchunk * 128 - v.shape[0]
        if pad:
            v = np.concatenate([v, np.zeros(pad, np.float32)])
        return v.reshape(nchunk, 128).T

    cols = np.zeros((128, NCOL), np.float32)
    mu = f("mu_shift")[0]
    cols[:, C_MU:C_MU + 27] = colfm(mu, 27)
    cols[:, C_W0:C_W0 + 8] = colfm(f("w0")[0], 8)
    cols[:, C_A0:C_A0 + 8] = colfm(f("a0")[0], 8)
    cols[:, C_KK:C_KK + 8] = colfm(f("k_k")[0], 8)
    cols[:, C_KA:C_KA + 8] = colfm(f("k_a")[0], 8)
    cols[:, C_RK:C_RK + 8] = colfm(f("r_k")[0], 8)
    cols[:, C_LG:C_LG + 8] = colfm(f("lnx_g")[0], 8)
    cols[:, C_LB:C_LB + 8] = colfm(f("lnx_b")[0], 8)
    cols[:, C_BG:C_BG + 16] = colfm(f("b_gate")[0], 16)
    cs = f("conv_sc")[0]
    for tap in range(3):
        cols[:, C_CS + tap * 8:C_CS + tap * 8 + 8] = colfm(cs[tap], 8)
    cf = f("conv_ffn")[0]
    for tap in range(3):
        cols[:, C_CF + tap * 44:C_CF + tap * 44 + 44] = colfm(cf[tap], 44)
    cols[:, C_G1:C_G1 + 8] = colfm(f("norm1_g")[0], 8)
    cols[:, C_G2:C_G2 + 8] = colfm(f("norm2_g")[0], 8)
    gbc = np.ascontiguousarray(np.broadcast_to(f("final_norm_g"), (128, D)), dtype=np.float32)
    lora_da = np.concatenate([f("w_decay_up")[0], f("w_aaa_up")[0]], axis=0)
    wg = f("w_gate_up")[0]
    lora_pk = np.zeros((8, 128, 3, 128), np.float32)
    for m_ in range(8):
        lora_pk[m_, :, 0, :] = lora_da[:, m_ * 128:(m_ + 1) * 128]
        lora_pk[m_, :, 1, :] = wg[0:128, m_ * 128:(m_ + 1) * 128]
        lora_pk[m_, 0:32, 2, :] = wg[128:160, m_ * 128:(m_ + 1) * 128]

    shared = {
        "meta": f("meta_tokens"), "w_in": f("w_in")[0], "w_br": f("w_branch_rwkv")[0],
        "w_bs": f("w_branch_sc")[0], "w_out": f("w_out")[0], "w_up": f("w_up")[0], "w_down": f("w_down")[0],
        "lora_pk": lora_pk, "cols": cols, "gbc": gbc, "ident": ident, "bones": bones,
        "maska": maska, "rmask": rmask, "maskas": maskas, "segmask": segmask,
    }
    st_wkv, st_shift = f("state_wkv")[0], f("state_shift")[0]
    st_sc, st_ffn = f("state_sc_conv")[0], f("state_ffn_conv")[0]
    in_maps = []
    for c in range(8):
        sl_ = slice(16 * c, 16 * c + 16)
        m = dict(shared)
        m["xp"] = x_prompt[c]
        m["xs"] = np.ascontiguousarray(x_sample[sl_].reshape(64, D))
        m["st_wkv"] = np.ascontiguousarray(st_wkv[sl_])
        m["st_shift"] = np.ascontiguousarray(st_shift[sl_])
        m["st_sc"] = np.ascontiguousarray(st_sc[sl_].reshape(32, D))
        m["st_ffn"] = np.ascontiguousarray(st_ffn[sl_].reshape(32, 2 * DFF))
        in_maps.append(m)

    if "nc" not in _NC_CACHE:
        _, order = build_program(None)
        assert len(order) == 480 and all(order[j] == order[j % 80] for j in range(480))
        _NC_CACHE["nc"], order2 = build_program(order)
        assert order == order2
    nc = _NC_CACHE["nc"]
    res = run_bass_kernel_spmd(nc, in_maps, core_ids=list(range(8)))
    R = res.results
    y_prompt = np.stack([R[c]["y_p"] for c in range(8)], 0)
    y_sample = np.concatenate([R[c]["y_s"].reshape(16, 4, D) for c in range(8)], 0)
    wkv_p = np.stack([R[c]["wkv_p"] for c in range(8)], 0)[None]
    wkv_s = np.concatenate([R[c]["wkv_s"] for c in range(8)], 0)[None]
    shift_p = np.stack([R[c]["shift_o"][16] for c in range(8)], 0)[None]
    shift_s = np.concatenate([R[c]["shift_o"][0:16] for c in range(8)], 0)[None]
    sc_p = np.stack([R[c]["sc_o"][32:34] for c in range(8)], 0)[None]
    sc_s = np.concatenate([R[c]["sc_o"][0:32].reshape(16, 2, D) for c in range(8)], 0)[None]
    ffn_p = np.stack([R[c]["ffn_o"][32:34] for c in range(8)], 0)[None]
    ffn_s = np.concatenate([R[c]["ffn_o"][0:32].reshape(16, 2, 2 * DFF) for c in range(8)], 0)[None]
    outs = (y_prompt, y_sample, wkv_p, wkv_s, shift_p, shift_s, sc_p, sc_s, ffn_p, ffn_s)
    return tuple(np.ascontiguousarray(o, dtype=np.float32) for o in outs)
```
